# Optimizing a Trainium2 kernel written in Bass

```python
import math
import jax
import jax.numpy as jnp
from jax import lax
import numpy as np

D_MODEL = 1024
BATCH = 1
SEQ = 16384
DEPTH = 4

N_MIXERS = 3
GLA_HEADS = 4
GLA_DK = D_MODEL // 2 // GLA_HEADS
GLA_DV = D_MODEL // GLA_HEADS
GLA_RANK = 16
GLA_NORMALIZER = 16.0
GLA_CHUNK = 64
GLA_IN = 2 * GLA_HEADS * GLA_DK + 2 * GLA_HEADS * GLA_DV + GLA_RANK
GDN_HEADS = 8
GDN_DK = D_MODEL // GDN_HEADS
GDN_DV = D_MODEL // GDN_HEADS
GDN_CONV = 4
GDN_CHUNK = 64
GDN_CONV_CH = 2 * GDN_HEADS * GDN_DK + GDN_HEADS * GDN_DV
GDN_IN = GDN_CONV_CH + GDN_HEADS * GDN_DV + 2 * GDN_HEADS
SG_WIDTH = D_MODEL
SG_GROUPS = 8
SG_CHUNK = 128
D_FF = ((8 * D_MODEL // 3 + 127) // 128) * 128
FFN_CONV = 3
LN_EPS = 1e-5
RMS_EPS = 1e-6
ALPHA = (2 * DEPTH) ** 0.25
BETA = (8 * DEPTH) ** -0.25

N_GLA_LAYERS = len(range(0, DEPTH, N_MIXERS))
N_GDN_LAYERS = len(range(1, DEPTH, N_MIXERS))
N_SGU_LAYERS = len(range(2, DEPTH, N_MIXERS))

kernel_name = "hybrid_gla_gdn_sgu_convffn_deepnorm"


def layer_norm(x, g, b):
    xf = x.astype(jnp.float32)
    mu = jnp.mean(xf, -1, keepdims=True)
    var = jnp.mean(jnp.square(xf - mu), -1, keepdims=True)
    return ((xf - mu) * lax.rsqrt(var + LN_EPS)).astype(x.dtype) * g + b


def rms_norm(x, w):
    xf = x.astype(jnp.float32)
    return xf * lax.rsqrt(jnp.mean(jnp.square(xf), -1, keepdims=True) + RMS_EPS) * w.astype(jnp.float32)


def l2_normalize(x):
    return x * lax.rsqrt(jnp.sum(jnp.square(x), -1, keepdims=True) + RMS_EPS)


def causal_dwconv(x, w):
    k, c = w.shape
    return lax.conv_general_dilated(
        x, w[:, None, :].astype(x.dtype), window_strides=(1,), padding=((k - 1, 0),),
        dimension_numbers=("NWC", "WIO", "NWC"), feature_group_count=c)


def to_head_chunks(x, n_heads, chunk):
    b, s, _ = x.shape
    return x.reshape(b, s // chunk, chunk, n_heads, -1).transpose(0, 3, 1, 2, 4)


def scalar_head_chunks(x, chunk):
    b, s, h = x.shape
    return x.reshape(b, s // chunk, chunk, h).transpose(0, 3, 1, 2)


def from_head_chunks(o):
    b, h, n, c, d = o.shape
    return o.transpose(0, 2, 3, 1, 4).reshape(b, n * c, h * d)


def gla_chunked(q, k, v, g):
    c = q.shape[-2]
    bcum = jnp.cumsum(g, axis=-2)
    b_last = bcum[..., -1:, :]
    b_ref = bcum[..., c // 2:c // 2 + 1, :]
    causal = jnp.tril(jnp.ones((c, c), dtype=bool))
    scores = jnp.einsum('bhnik,bhnjk->bhnij', q * jnp.exp(bcum - b_ref), k * jnp.exp(b_ref - bcum))
    o_intra = jnp.einsum('bhnij,bhnjv->bhniv', jnp.where(causal, scores, 0.0), v)
    d_state = jnp.einsum('bhnck,bhncv->bhnkv', k * jnp.exp(b_last - bcum), v)
    chunk_decay = jnp.exp(b_last[..., 0, :])

    def step(state, inp):
        dec, ds = inp
        return state * dec[..., None] + ds, state

    bsz, h, _, _, dk = q.shape
    s0 = jnp.zeros((bsz, h, dk, v.shape[-1]), jnp.float32)
    _, s_start = lax.scan(step, s0, (jnp.moveaxis(chunk_decay, 2, 0), jnp.moveaxis(d_state, 2, 0)))
    s_start = jnp.moveaxis(s_start, 0, 2)
    o_inter = jnp.einsum('bhnck,bhnkv->bhncv', q * jnp.exp(bcum), s_start)
    return o_intra + o_inter


def gla_mixer(x, w_in, w_gk2, b_gk, norm_w, w_out):
    hk, hv = GLA_HEADS * GLA_DK, GLA_HEADS * GLA_DV
    proj = x @ w_in
    q, k, v, gate, gk_lr = jnp.split(proj, [hk, 2 * hk, 2 * hk + hv, 2 * hk + 2 * hv], axis=-1)
    gk = jax.nn.log_sigmoid((gk_lr @ w_gk2 + b_gk).astype(jnp.float32)) / GLA_NORMALIZER
    f32 = jnp.float32
    qc = to_head_chunks(q.astype(f32) * GLA_DK ** -0.5, GLA_HEADS, GLA_CHUNK)
    kc = to_head_chunks(k.astype(f32), GLA_HEADS, GLA_CHUNK)
    vc = to_head_chunks(v.astype(f32), GLA_HEADS, GLA_CHUNK)
    gc = to_head_chunks(gk, GLA_HEADS, GLA_CHUNK)
    o = rms_norm(gla_chunked(qc, kc, vc, gc), norm_w)
    o = from_head_chunks(o).astype(x.dtype)
    return (o * jax.nn.silu(gate)) @ w_out


def gdn_chunked(q, k, v, beta, g):
    c = q.shape[-2]
    bcum = jnp.cumsum(g, axis=-1)
    incl = jnp.tril(jnp.ones((c, c), dtype=bool))
    strict = jnp.tril(jnp.ones((c, c), dtype=bool), -1)
    diff = bcum[..., :, None] - bcum[..., None, :]
    decay_ij = jnp.exp(jnp.where(incl, diff, -jnp.inf))
    k_beta = k * beta[..., None]
    m = jnp.where(strict, jnp.einsum('bhnik,bhnjk->bhnij', k_beta, k) * decay_ij, 0.0)
    rhs = jnp.concatenate([v * beta[..., None], k_beta * jnp.exp(bcum)[..., None]], axis=-1)
    sol = lax.linalg.triangular_solve(m, rhs, left_side=True, lower=True, unit_diagonal=True)
    u, w = jnp.split(sol, [v.shape[-1]], axis=-1)
    qk = jnp.einsum('bhnik,bhnjk->bhnij', q, k) * decay_ij
    q_dec = q * jnp.exp(bcum)[..., None]
    k_dec = k * jnp.exp(bcum[..., -1:] - bcum)[..., None]
    chunk_decay = jnp.exp(bcum[..., -1])

    def step(state, inp):
        qk_n, u_n, w_n, qd_n, kd_n, cd_n = inp
        v_new = u_n - jnp.einsum('bhck,bhkv->bhcv', w_n, state)
        o_n = jnp.einsum('bhck,bhkv->bhcv', qd_n, state) + jnp.einsum('bhij,bhjv->bhiv', qk_n, v_new)
        state = state * cd_n[..., None, None] + jnp.einsum('bhck,bhcv->bhkv', kd_n, v_new)
        return state, o_n

    bsz, h, _, _, dk = q.shape
    s0 = jnp.zeros((bsz, h, dk, v.shape[-1]), jnp.float32)
    xs = tuple(jnp.moveaxis(a, 2, 0) for a in (qk, u, w, q_dec, k_dec, chunk_decay))
    _, o = lax.scan(step, s0, xs)
    return jnp.moveaxis(o, 0, 2)


def gdn_mixer(x, w_in, conv_w, a_log, dt_bias, norm_w, w_out):
    hk, hv = GDN_HEADS * GDN_DK, GDN_HEADS * GDN_DV
    proj = x @ w_in
    qkv, gate, beta_lin, a_lin = jnp.split(
        proj, [GDN_CONV_CH, GDN_CONV_CH + hv, GDN_CONV_CH + hv + GDN_HEADS], axis=-1)
    qkv = jax.nn.silu(causal_dwconv(qkv, conv_w))
    q, k, v = jnp.split(qkv.astype(jnp.float32), [hk, 2 * hk], axis=-1)
    qc = l2_normalize(to_head_chunks(q, GDN_HEADS, GDN_CHUNK)) * GDN_DK ** -0.5
    kc = l2_normalize(to_head_chunks(k, GDN_HEADS, GDN_CHUNK))
    vc = to_head_chunks(v, GDN_HEADS, GDN_CHUNK)
    beta = scalar_head_chunks(jax.nn.sigmoid(beta_lin.astype(jnp.float32)), GDN_CHUNK)
    g = -jnp.exp(a_log.astype(jnp.float32)) * jax.nn.softplus(
        a_lin.astype(jnp.float32) + dt_bias.astype(jnp.float32))
    gc = scalar_head_chunks(g, GDN_CHUNK)
    o = rms_norm(gdn_chunked(qc, kc, vc, beta, gc), norm_w)
    o = from_head_chunks(o).astype(x.dtype)
    return (o * jax.nn.silu(gate)) @ w_out


def sgu_mixer(x, w_in, ln_g, ln_b, w_sp, b_sp, w_out):
    z = jax.nn.gelu(x @ w_in, approximate=False)
    u, v = jnp.split(z, 2, axis=-1)
    v = layer_norm(v, ln_g, ln_b)
    bsz, s, _ = v.shape
    v = v.reshape(bsz, s // SG_CHUNK, SG_CHUNK, SG_GROUPS, SG_WIDTH // SG_GROUPS)
    causal = jnp.tril(jnp.ones((SG_CHUNK, SG_CHUNK), dtype=bool))
    w_causal = jnp.where(causal, w_sp, 0.0).astype(v.dtype)
    mixed = jnp.einsum('gts,bnsgd->bntgd', w_causal, v) + b_sp.T[:, :, None]
    return (u * mixed.reshape(bsz, s, SG_WIDTH)) @ w_out


def conv_ffn(x, w_in, conv_w, w_out):
    h = causal_dwconv(x @ w_in, conv_w)
    gate, up = jnp.split(h, 2, axis=-1)
    return (jax.nn.gelu(gate, approximate=False) * up) @ w_out


def setup_inputs(seed: int = 0) -> dict:
    key = jax.random.key(seed)
    ks = jax.random.split(key, 24)
    f32 = jnp.float32

    def nrm(k, shape, scale):
        return jax.random.normal(k, shape, f32) * scale

    na, nb, nc = N_GLA_LAYERS, N_GDN_LAYERS, N_SGU_LAYERS
    x = nrm(ks[0], (BATCH, SEQ, D_MODEL), 1.0)
    gla_w_in = nrm(ks[1], (na, D_MODEL, GLA_IN), D_MODEL ** -0.5)
    gla_w_gk2 = nrm(ks[2], (na, GLA_RANK, GLA_HEADS * GLA_DK), GLA_RANK ** -0.5)
    gla_b_gk = nrm(ks[3], (na, GLA_HEADS * GLA_DK), 0.01)
    gla_norm_w = 1.0 + nrm(ks[4], (na, GLA_DV), 0.01)
    gla_w_out = nrm(ks[5], (na, GLA_HEADS * GLA_DV, D_MODEL), (GLA_HEADS * GLA_DV) ** -0.5 * BETA)
    gdn_w_in = nrm(ks[6], (nb, D_MODEL, GDN_IN), D_MODEL ** -0.5)
    gdn_conv_w = nrm(ks[7], (nb, GDN_CONV, GDN_CONV_CH), GDN_CONV ** -0.5)
    gdn_a_log = jnp.log(jax.random.uniform(ks[8], (nb, GDN_HEADS), f32, 1.0, 16.0))
    dt = jnp.exp(jax.random.uniform(ks[9], (nb, GDN_HEADS), f32, math.log(1e-3), math.log(1e-1)))
    gdn_dt_bias = dt + jnp.log(-jnp.expm1(-dt))
    gdn_norm_w = 1.0 + nrm(ks[10], (nb, GDN_DV), 0.01)
    gdn_w_out = nrm(ks[11], (nb, GDN_HEADS * GDN_DV, D_MODEL), (GDN_HEADS * GDN_DV) ** -0.5 * BETA)
    sg_w_in = nrm(ks[12], (nc, D_MODEL, 2 * SG_WIDTH), D_MODEL ** -0.5)
    sg_ln_g = 1.0 + nrm(ks[13], (nc, SG_WIDTH), 0.01)
    sg_ln_b = nrm(ks[14], (nc, SG_WIDTH), 0.01)
    sg_w_sp = nrm(ks[15], (nc, SG_GROUPS, SG_CHUNK, SG_CHUNK), 0.5 * SG_CHUNK ** -0.5)
    sg_b_sp = 1.0 + nrm(ks[16], (nc, SG_GROUPS, SG_CHUNK), 0.01)
    sg_w_out = nrm(ks[17], (nc, SG_WIDTH, D_MODEL), SG_WIDTH ** -0.5 * BETA)
    ffn_w_in = nrm(ks[18], (DEPTH, D_MODEL, 2 * D_FF), D_MODEL ** -0.5)
    ffn_conv_w = nrm(ks[19], (DEPTH, FFN_CONV, 2 * D_FF), FFN_CONV ** -0.5)
    ffn_w_out = nrm(ks[20], (DEPTH, D_FF, D_MODEL), D_FF ** -0.5 * BETA)
    ln_g = 1.0 + nrm(ks[21], (DEPTH, 2, D_MODEL), 0.01)
    ln_b = nrm(ks[22], (DEPTH, 2, D_MODEL), 0.01)
    return {"x": x,
            "gla_w_in": gla_w_in, "gla_w_gk2": gla_w_gk2, "gla_b_gk": gla_b_gk,
            "gla_norm_w": gla_norm_w, "gla_w_out": gla_w_out,
            "gdn_w_in": gdn_w_in, "gdn_conv_w": gdn_conv_w, "gdn_a_log": gdn_a_log,
            "gdn_dt_bias": gdn_dt_bias, "gdn_norm_w": gdn_norm_w, "gdn_w_out": gdn_w_out,
            "sg_w_in": sg_w_in, "sg_ln_g": sg_ln_g, "sg_ln_b": sg_ln_b,
            "sg_w_sp": sg_w_sp, "sg_b_sp": sg_b_sp, "sg_w_out": sg_w_out,
            "ffn_w_in": ffn_w_in, "ffn_conv_w": ffn_conv_w, "ffn_w_out": ffn_w_out,
            "ln_g": ln_g, "ln_b": ln_b}


def reference(x, gla_w_in, gla_w_gk2, gla_b_gk, gla_norm_w, gla_w_out,
              gdn_w_in, gdn_conv_w, gdn_a_log, gdn_dt_bias, gdn_norm_w, gdn_w_out,
              sg_w_in, sg_ln_g, sg_ln_b, sg_w_sp, sg_b_sp, sg_w_out,
              ffn_w_in, ffn_conv_w, ffn_w_out, ln_g, ln_b):
    h = x
    for i in range(DEPTH):
        mixer, j = i % N_MIXERS, i // N_MIXERS
        if mixer == 0:
            y = gla_mixer(h, gla_w_in[j], gla_w_gk2[j], gla_b_gk[j], gla_norm_w[j], gla_w_out[j])
        elif mixer == 1:
            y = gdn_mixer(h, gdn_w_in[j], gdn_conv_w[j], gdn_a_log[j], gdn_dt_bias[j],
                          gdn_norm_w[j], gdn_w_out[j])
        else:
            y = sgu_mixer(h, sg_w_in[j], sg_ln_g[j], sg_ln_b[j], sg_w_sp[j], sg_b_sp[j], sg_w_out[j])
        h = layer_norm(ALPHA * h + y, ln_g[i, 0], ln_b[i, 0])
        h = layer_norm(ALPHA * h + conv_ffn(h, ffn_w_in[i], ffn_conv_w[i], ffn_w_out[i]),
                       ln_g[i, 1], ln_b[i, 1])
    return h
```

```python
import contextlib
import numpy as np
import concourse.bass as bass
import concourse.mybir as mybir
from concourse.bass_utils import run_bass_kernel_spmd

F32 = mybir.dt.float32
BF16 = mybir.dt.bfloat16
AF = mybir.ActivationFunctionType
ALU = mybir.AluOpType
AX = mybir.AxisListType

NCORES = 8
SEQ = 16384
D = 1024
TPC = SEQ // NCORES
NT = TPC // 128
KC = D // 128
DFF = 2816
NFC = DFF // 128
DEPTH = 4
ALPHA = (2 * DEPTH) ** 0.25
LN_EPS = 1e-5
RMS_EPS = 1e-6


class FW:
    CE = ("pe", "dve", "act", "pool", "sp")
    NL = 6

    def __init__(self, nc, es):
        self.nc = nc
        self.eng = dict(pe=nc.tensor, dve=nc.vector, act=nc.scalar, pool=nc.gpsimd, sp=nc.sync)
        self.sem = {}
        self.cnt = {}
        for e in self.CE:
            self.sem[e] = es.enter_context(nc.semaphore("s_" + e))
            self.cnt[e] = 0
        self.lanes = {}
        for q in ("sp", "pool", "act"):
            self.lanes[q] = []
            for i in range(self.NL):
                r = "dma_%s_%d" % (q, i)
                self.sem[r] = es.enter_context(nc.semaphore("s_" + r))
                self.cnt[r] = 0
                self.lanes[q].append(r)
        self.lane_next = {q: 0 for q in self.lanes}
        self.sem["cc"] = es.enter_context(nc.semaphore("s_cc"))
        self.cnt["cc"] = 0
        self.clock = {e: {} for e in self.CE}
        self.W = {}
        self.R = {}

    def _deps(self, reads, writes):
        deps = {}
        for k in reads:
            w = self.W.get(k)
            if w is not None:
                deps[w[0]] = max(deps.get(w[0], 0), w[1])
        for k in writes:
            w = self.W.get(k)
            if w is not None:
                deps[w[0]] = max(deps.get(w[0], 0), w[1])
            for r, v in self.R.get(k, {}).items():
                deps[r] = max(deps.get(r, 0), v)
        return deps

    def _wait(self, e, deps, skip_self=False):
        for r, v in deps.items():
            if skip_self and r == e:
                continue
            if self.clock[e].get(r, 0) >= v:
                continue
            self.eng[e].wait_ge(self.sem[r], v)
            self.clock[e][r] = v

    def _mark(self, res, val, reads, writes):
        for k in writes:
            self.W[k] = (res, val)
            self.R[k] = {}
        for k in reads:
            self.R.setdefault(k, {})[res] = val

    def op(self, e, fn, reads=(), writes=()):
        deps = self._deps(reads, writes)
        self._wait(e, deps, skip_self=(e == "pe"))
        ins = fn(self.eng[e])
        self.cnt[e] += 1
        ins.then_inc(self.sem[e], 1)
        self._mark(e, self.cnt[e], reads, writes)
        return ins

    def dma(self, q, out, in_, reads=(), writes=()):
        deps = self._deps(reads, writes)
        self._wait(q, deps)
        lane = self.lanes[q][self.lane_next[q]]
        self.lane_next[q] = (self.lane_next[q] + 1) % self.NL
        if self.cnt[lane] > 0:
            self._wait(q, {lane: self.cnt[lane]})
        ins = self.eng[q].dma_start(out=out, in_=in_)
        self.cnt[lane] += 16
        ins.then_inc(self.sem[lane], 16)
        self._mark(lane, self.cnt[lane], reads, writes)
        return ins

    def allgather(self, in_ap, out_ap, reads=(), writes=()):
        deps = self._deps(reads, writes)
        self._wait("pool", deps)
        if self.cnt["cc"] > 0:
            self._wait("pool", {"cc": self.cnt["cc"]})
        ins = self.nc.gpsimd.collective_compute(
            "AllGather", ALU.bypass, replica_groups=[list(range(NCORES))],
            ins=[in_ap], outs=[out_ap])
        self.cnt["cc"] += 1
        ins.then_inc(self.sem["cc"], 1)
        self._mark("cc", self.cnt["cc"], reads, writes)
        self._wait("pool", {"cc": self.cnt["cc"]})

    def barrier(self):
        allr = {r: v for r, v in self.cnt.items() if v > 0}
        for e in self.CE:
            self._wait(e, dict(allr))

    def finish(self, e="sp"):
        self._wait(e, {r: v for r, v in self.cnt.items() if v > 0})


_UID = [0]


def _uid():
    _UID[0] += 1
    return _UID[0]


class _Stop(Exception):
    pass


class Builder:
    def __init__(self, stages):
        self.stages = stages
        self.stop = 0
        self.stopped = False
        for a in stages:
            if a.startswith("stop"):
                self.stop = int(a[4:])

    def _init2(self):
        self.nc = bass.Bass("TRN2", target_bir_lowering=False)
        self.din = {}

    def _suppress(self, et, ev, tb):
        if et is _Stop:
            self.stopped = True
            return True
        return False

    def chk(self, k):
        if self.stop == k:
            raise _Stop()

    def _unused(self):
        pass

    def inp(self, name, shape, dt=F32):
        t = self.nc.dram_tensor(name, list(shape), dt, kind="ExternalInput")
        self.din[name] = t
        return t

    def build(self):
        self._init2()
        nc = self.nc
        x_d = self.inp("x", [TPC, D])
        ffn_w_in = self.inp("ffn_w_in", [DEPTH, D, 2 * DFF])
        ffn_w_out = self.inp("ffn_w_out", [DEPTH, DFF, D])
        ffn_cw = self.inp("ffn_cw", [DEPTH, 128, 2 * NFC, 3])
        ln_g = self.inp("ln_g", [DEPTH * 2, D])
        ln_b = self.inp("ln_b", [DEPTH * 2, D])
        ident_d = self.inp("ident", [128, 128])
        hmask_d = self.inp("hmask", [128, NCORES, 32])
        self.inp("triu", [128, 128])
        sg_w_in = self.inp("sg_w_in", [1, D, 2 * D])
        sg_w_out = self.inp("sg_w_out", [1, D, D])
        sg_wspT = self.inp("sg_wspT", [1, 128, 8, 128])
        sg_bspT = self.inp("sg_bspT", [1, 128, 8])
        sg_ln_g = self.inp("sg_ln_g", [1, D])
        sg_ln_b = self.inp("sg_ln_b", [1, D])
        self.inp("cmask", [128, NCORES])
        self.inp("gla_w_in", [2, D, 3088])
        self.inp("gla_w_gk2", [2, 16, 512])
        self.inp("gla_b_gkT", [2, 128, 4])
        self.inp("gla_norm_wt", [2, D])
        self.inp("gla_w_out", [2, D, D])
        self.inp("trilS", [128, 128])
        self.inp("gdn_w_in", [1, D, 4112])
        self.inp("gdn_cw", [1, 128, 24, 4])
        self.inp("gdn_a_log", [1, 8])
        self.inp("gdn_dt_bias", [1, 8])
        self.inp("gdn_norm_wt", [1, D])
        self.inp("gdn_w_out", [1, D, D])
        gd_in = nc.dram_tensor("gd_in", [128, 2048], F32, kind="Internal")
        gd_all = nc.dram_tensor("gd_all", [NCORES * 128, 2048], F32, kind="Internal", addr_space="Shared")
        rtscr = nc.dram_tensor("rtscr", [128, 8, TPC], BF16, kind="Internal")
        oscr = nc.dram_tensor("oscr", [TPC, D], F32, kind="Internal")
        gst_in = nc.dram_tensor("gst_in", [128, 1028], F32, kind="Internal")
        gst_all = nc.dram_tensor("gst_all", [NCORES * 128, 1028], F32, kind="Internal", addr_space="Shared")
        out_d = nc.dram_tensor("out", [TPC, D], F32, kind="ExternalOutput")
        halo_in = nc.dram_tensor("halo_in", [128, 32], BF16, kind="Internal")
        halo_all = nc.dram_tensor("halo_all", [NCORES * 128, 32], BF16, kind="Internal", addr_space="Shared")
        self.d = dict(x=x_d, ffn_w_in=ffn_w_in, ffn_w_out=ffn_w_out, ffn_cw=ffn_cw, ln_g=ln_g, ln_b=ln_b,
                      out=out_d, halo_in=halo_in, halo_all=halo_all)
        self.d.update({k: v for k, v in self.din.items()})
        self.d.update(oscr=oscr, gst_in=gst_in, gst_all=gst_all, gd_in=gd_in, gd_all=gd_all, rtscr=rtscr)

        with contextlib.ExitStack() as es:
            self.es = es
            sb = lambda name, shape, dt: es.enter_context(nc.sbuf_tensor("sb_%s_%d" % (name, _uid()), list(shape), dt))
            ps = lambda name, shape, dt: es.enter_context(nc.psum_tensor("ps_" + name, list(shape), dt))
            self.H = sb("H", [128, NT, D], F32)
            self.XT = sb("XT", [128, KC, TPC], BF16)
            self.XTH = sb("XTH", [128, KC, 4], BF16)
            self.HA = sb("HA", [128, NCORES, 32], BF16)
            self.HAm = sb("HAm", [128, NCORES, 32], F32)
            self.XTHf = sb("XTHf", [128, 32], F32)
            self.hmask = sb("hmask", [128, NCORES, 32], BF16)
            self.ident = sb("ident", [128, 128], BF16)
            self.identf = sb("identf", [128, 128], F32)
            self.lng = sb("lng", [128, D], F32)
            self.lnb = sb("lnb", [128, D], F32)
            self.xb = sb("xb", [128, D], BF16)
            self.srcT = sb("srcT", [128, KC, 128], BF16)
            self.triu = sb("triu", [128, 128], F32)
            self.cmask = sb("cmask", [128, NCORES], F32)
            self.ones = sb("ones", [128, 512], F32)
            self.onesf = sb("onesf", [128, 128], F32)
            self.trilS = sb("trilS", [128, 128], F32)
            self.zt = sb("zt", [128, D], F32)
            self.st6 = sb("st6", [128, 12], F32)
            self.mv = sb("mv", [128, 2], F32)
            self.rstd = sb("rstd", [128, 1], F32)
            self.psA = [ps("psA%d" % i, [128, 512], F32) for i in range(4)]
            self.psO = ps("psO", [128, 1024], F32)
            self.psT = ps("psT", [128, KC, 128], BF16)
            self.psM = ps("psM", [128, 512], F32)
            es.enter_context(nc.Block())
            self.fw = FW(nc, es)
            self.body()
            self.fw.finish("sp")
        return nc

    def body(self):
        fw = self.fw
        d = self.d
        fw.dma("pool", self.ident[:], self.din["ident"].ap(), writes=["ident"])
        fw.dma("sp", self.identf[:], self.din["ident"].ap(), writes=["identf"])
        fw.dma("pool", self.hmask[:], self.din["hmask"].ap(), writes=["hmask"])
        fw.dma("sp", self.triu[:], self.din["triu"].ap(), writes=["triu"])
        fw.dma("sp", self.cmask[:], self.din["cmask"].ap(), writes=["cmask"])
        fw.op("dve", lambda e: e.memset(self.ones[:], 1.0), writes=["ones"])
        fw.op("dve", lambda e: e.memset(self.onesf[:], 1.0), writes=["onesf"])
        fw.dma("sp", self.trilS[:], self.din["trilS"].ap(), writes=["trilS"])
        xv = d["x"].ap().rearrange("(t p) f -> p t f", p=128)
        for t in range(NT):
            fw.dma("sp", self.H[:, t, :], xv[:, t, :], writes=[("H", t)])
        st = self.stages
        for t in range(NT):
            self.make_xt(t)
        if "ffn_only" in st:
            self.halo_exchange()
            self.ffn(0)
        else:
            layers = range(DEPTH) if "full" in st else [int(a[1:]) for a in st if a.startswith("L")]
            for li in layers:
                mixer, j = li % 3, li // 3
                if mixer == 0:
                    self.gla(li, j)
                elif mixer == 1:
                    self.halo_exchange()
                    self.gdn(li, j)
                    if self.stopped:
                        self.fw.barrier()
                else:
                    self.sgu(li, j)
                self.halo_exchange()
                self.ffn(li)
        ov = d["out"].ap().rearrange("(t p) f -> p t f", p=128)
        for t in range(NT):
            fw.dma("sp", ov[:, t, :], self.H[:, t, :], reads=[("H", t)], writes=[("out", t)])

    def make_xt(self, t):
        fw = self.fw
        fw.op("act", lambda e: e.copy(self.xb[:], self.H[:, t, :]), reads=[("H", t)], writes=["xb"])
        for kc in range(KC):
            fw.op("pe", lambda e, kc=kc: e.transpose(self.psT[:, kc, :], self.xb[:, kc * 128:(kc + 1) * 128], self.ident[:]),
                  reads=["xb", "ident"], writes=["psT"])
        fw.op("act", lambda e: e.copy(self.XT[:, :, t * 128:(t + 1) * 128], self.psT[:]),
              reads=["psT"], writes=[("XT", t)])

    def halo_exchange(self):
        fw = self.fw
        d = self.d
        fw.dma("sp", d["halo_in"].ap().rearrange("p (k f) -> p k f", k=KC), self.XT[:, :, TPC - 4:TPC],
               reads=[("XT", NT - 1)], writes=["halo_in"])
        fw.allgather(d["halo_in"].ap(), d["halo_all"].ap(), reads=["halo_in"], writes=["halo_all"])
        fw.dma("pool", self.HA[:], d["halo_all"].ap().rearrange("(c p) f -> p c f", p=128),
               reads=["halo_all"], writes=["HA"])
        fw.op("dve", lambda e: e.tensor_tensor(self.HAm[:], self.HA[:], self.hmask[:], ALU.mult),
              reads=["HA", "hmask"], writes=["HAm"])
        fw.op("dve", lambda e: e.tensor_reduce(self.XTHf[:], self.HAm[:].rearrange("p c f -> p f c"), AX.X, ALU.add),
              reads=["HAm"], writes=["XTHf"])
        fw.op("dve", lambda e: e.tensor_copy(self.XTH[:].rearrange("p k f -> p (k f)"), self.XTHf[:]),
              reads=["XTHf"], writes=["XTH"])

    def layer_norm_tile(self, t, ps_y, li, ykeys):
        fw = self.fw
        fw.op("dve", lambda e: e.scalar_tensor_tensor(self.zt[:], self.H[:, t, :], ALPHA, ps_y, ALU.mult, ALU.add),
              reads=[("H", t)] + ykeys, writes=["zt"])
        for c in range(2):
            fw.op("dve", lambda e, c=c: e.bn_stats(self.st6[:, c * 6:(c + 1) * 6], self.zt[:, c * 512:(c + 1) * 512]),
                  reads=["zt"], writes=[("st6", c)])
        fw.op("dve", lambda e: e.bn_aggr(self.mv[:], self.st6[:]), reads=[("st6", 0), ("st6", 1)], writes=["mv"])
        fw.op("act", lambda e: e.activation(self.rstd[:], self.mv[:, 1:2], AF.Sqrt, bias=LN_EPS, scale=1.0),
              reads=["mv"], writes=["rstd"])
        fw.op("dve", lambda e: e.reciprocal(self.rstd[:], self.rstd[:]), reads=["rstd"], writes=["rstd"])
        fw.op("dve", lambda e: e.tensor_scalar(self.zt[:], self.zt[:], self.mv[:, 0:1], self.rstd[:, 0:1],
                                               ALU.subtract, ALU.mult),
              reads=["zt", "mv", "rstd"], writes=["zt"])
        fw.op("pool", lambda e: e.tensor_tensor(self.zt[:], self.zt[:], self.lng[:], ALU.mult),
              reads=["zt", "lng"], writes=["zt"])
        fw.op("pool", lambda e: e.tensor_tensor(self.H[:, t, :], self.zt[:], self.lnb[:], ALU.add),
              reads=["zt", "lnb"], writes=[("H", t)])


    def out_proj_ln(self, t, src, srckeys, WOm, wkey, li):
        fw = self.fw
        for kc in range(KC):
            fw.op("pe", lambda e, kc=kc: e.transpose(self.psT[:, kc, :], src[:, kc * 128:(kc + 1) * 128], self.ident[:]),
                  reads=srckeys + ["ident"], writes=["psT"])
        fw.op("act", lambda e: e.copy(self.srcT[:], self.psT[:]), reads=["psT"], writes=["srcT"])
        for hf in range(2):
            for kc in range(KC):
                fw.op("pe", lambda e, kc=kc, hf=hf: e.matmul(
                    self.psO[:, hf * 512:(hf + 1) * 512], self.srcT[:, kc, :], WOm[:, kc, hf * 512:(hf + 1) * 512],
                    start=(kc == 0), stop=(kc == KC - 1)), reads=["srcT", wkey], writes=["psO"])
        self.layer_norm_tile(t, self.psO[:], li, ["psO"])
        self.make_xt(t)

    def load_wout(self, WOm, w_ap, wkey):
        v = w_ap.rearrange("(k p) n -> p k n", p=128)
        for kc in range(KC):
            self.fw.dma("pool", WOm[:, kc, :], v[:, kc, :], writes=[wkey] if kc == 0 else [(wkey, kc)])

    def sgu(self, li, j):
        nc = self.nc
        fw = self.fw
        d = self.d
        self.load_ln(li * 2)
        with contextlib.ExitStack() as es:
            sb = lambda name, shape, dt: es.enter_context(nc.sbuf_tensor("sb_%s_%d" % (name, _uid()), list(shape), dt))
            Wi = sb("sgWi", [128, KC, 2 * D], BF16)
            WOm = sb("sgWO", [128, KC, D], BF16)
            Wsp = sb("sgWsp", [128, 8, 128], BF16)
            Wspf = sb("sgWspf", [128, 8, 128], F32)
            bsp = sb("sgbsp", [128, 8], F32)
            sg_g = sb("sg_g", [128, D], F32)
            sg_b = sb("sg_b", [128, D], F32)
            u = sb("sgu", [128, D], F32)
            vz = sb("sgvz", [128, D], F32)
            vn = sb("sgvn", [128, D], BF16)
            um = sb("sgum", [128, D], BF16)
            wv = d["sg_w_in"].ap()[j].rearrange("(k p) n -> p k n", p=128)
            for kc in range(KC):
                fw.dma("pool", Wi[:, kc, :], wv[:, kc, :], writes=[("sgWi", kc)])
            self.load_wout(WOm, d["sg_w_out"].ap()[j], "sgWO")
            fw.dma("sp", Wspf[:], d["sg_wspT"].ap()[j], writes=["Wspf"])
            fw.dma("sp", bsp[:], d["sg_bspT"].ap()[j], writes=["bsp"])
            fw.dma("sp", sg_g[:], d["sg_ln_g"].ap()[j:j + 1, :].partition_broadcast(128), writes=["sg_g"])
            fw.dma("sp", sg_b[:], d["sg_ln_b"].ap()[j:j + 1, :].partition_broadcast(128), writes=["sg_b"])
            fw.op("dve", lambda e: e.tensor_tensor(Wsp[:], Wspf[:], self.triu[:].unsqueeze(1).broadcast_to([128, 8, 128]), ALU.mult),
                  reads=["Wspf", "triu"], writes=["Wsp"])
            wkeys = [("sgWi", kc) for kc in range(KC)]
            for t in range(NT):
                tsl = slice(t * 128, (t + 1) * 128)
                for cb in range(4):
                    for kc in range(KC):
                        fw.op("pe", lambda e, kc=kc, cb=cb: e.matmul(
                            self.psA[cb][:], self.XT[:, kc, tsl], Wi[:, kc, cb * 512:(cb + 1) * 512],
                            start=(kc == 0), stop=(kc == KC - 1)), reads=[("XT", t)] + wkeys, writes=[("psA", cb)])
                for cb in range(2):
                    fw.op("act", lambda e, cb=cb: e.activation(u[:, cb * 512:(cb + 1) * 512], self.psA[cb][:], AF.Gelu),
                          reads=[("psA", cb)], writes=[("sgu", cb)])
                    fw.op("act", lambda e, cb=cb: e.activation(vz[:, cb * 512:(cb + 1) * 512], self.psA[2 + cb][:], AF.Gelu),
                          reads=[("psA", 2 + cb)], writes=["sgvz"])
                for c in range(2):
                    fw.op("dve", lambda e, c=c: e.bn_stats(self.st6[:, c * 6:(c + 1) * 6], vz[:, c * 512:(c + 1) * 512]),
                          reads=["sgvz"], writes=[("st6", c)])
                fw.op("dve", lambda e: e.bn_aggr(self.mv[:], self.st6[:]), reads=[("st6", 0), ("st6", 1)], writes=["mv"])
                fw.op("act", lambda e: e.activation(self.rstd[:], self.mv[:, 1:2], AF.Sqrt, bias=LN_EPS, scale=1.0),
                      reads=["mv"], writes=["rstd"])
                fw.op("dve", lambda e: e.reciprocal(self.rstd[:], self.rstd[:]), reads=["rstd"], writes=["rstd"])
                fw.op("dve", lambda e: e.tensor_scalar(vz[:], vz[:], self.mv[:, 0:1], self.rstd[:, 0:1], ALU.subtract, ALU.mult),
                      reads=["sgvz", "mv", "rstd"], writes=["sgvz"])
                fw.op("pool", lambda e: e.tensor_tensor(vz[:], vz[:], sg_g[:], ALU.mult), reads=["sgvz", "sg_g"], writes=["sgvz"])
                fw.op("pool", lambda e: e.tensor_tensor(vn[:], vz[:], sg_b[:], ALU.add), reads=["sgvz", "sg_b"], writes=["sgvn"])
                for g in range(8):
                    fw.op("pe", lambda e, g=g: e.matmul(self.psA[g // 4][:, (g % 4) * 128:(g % 4 + 1) * 128],
                                                        Wsp[:, g, :], vn[:, g * 128:(g + 1) * 128], start=True, stop=True),
                          reads=["Wsp", "sgvn"], writes=[("psA", g // 4)])
                for g in range(8):
                    fw.op("dve", lambda e, g=g: e.scalar_tensor_tensor(
                        um[:, g * 128:(g + 1) * 128], self.psA[g // 4][:, (g % 4) * 128:(g % 4 + 1) * 128],
                        bsp[:, g:g + 1], u[:, g * 128:(g + 1) * 128], ALU.add, ALU.mult),
                        reads=[("psA", g // 4), "bsp", ("sgu", g // 4)], writes=["sgum"])
                self.out_proj_ln(t, um, ["sgum"], WOm, "sgWO", li * 2)
            fw.barrier()


    def gla(self, li, j):
        nc = self.nc
        fw = self.fw
        d = self.d
        NH, DK, DV = 4, 128, 256
        SC = DK ** -0.5
        TB = 256
        NCH = TB // 64
        self.load_ln(li * 2)
        with contextlib.ExitStack() as es_all:
            sba = lambda name, shape, dt: es_all.enter_context(nc.sbuf_tensor("sb_%s_%d" % (name, _uid()), list(shape), dt))
            QET = sba("QET", [128, NH, TPC], BF16)
            S0b = sba("S0b", [128, NH * DV], BF16)
            with contextlib.ExitStack() as es:
                sb = lambda name, shape, dt: es.enter_context(nc.sbuf_tensor("sb_%s_%d" % (name, _uid()), list(shape), dt))
                Wqk = sb("glWqk", [128, KC, 1024], BF16)
                Wv = sb("glWv", [128, KC, 1024], BF16)
                Wlr = sb("glWlr", [128, KC, 16], BF16)
                Wgk2 = sb("glWgk2", [16, 512], BF16)
                nbg = sb("glnbg", [128, NH], F32)
                lrT = sb("gllrT", [16, TB], BF16)
                e1 = sb("gle1", [128, TB], F32)
                sp = sb("glsp", [128, TB], F32)
                cum = sb("glcum", [128, NH, TB + 1], F32)
                nb = sb("glnb", [128, TB], F32)
                dref = sb("gldref", [128, TB], F32)
                dlast = sb("gldlast", [128, TB], F32)
                E = [sb("glE%d" % i, [128, TB], F32) for i in range(5)]
                dec = sb("gldec", [128, NH, NCH], F32)
                qp = sb("glqp", [128, NH, TB], BF16)
                qpp = sb("glqpp", [128, NH, TB], BF16)
                kp = sb("glkp", [128, NH, TB], BF16)
                kpp = sb("glkpp", [128, NH, TB], BF16)
                vb = sb("glvb", [64, NH * DV], BF16)
                AT = sb("glAT", [64, NH, 64], BF16)
                kT = sb("glkT", [64, NH, 128], BF16)
                oloc = sb("gloloc", [64, NH * DV], F32)
                S = sb("glS", [128, NH * DV], F32)
                Sb = sb("glSb", [128, NH * DV], BF16)
                dtot = sb("gldtot", [128, NH], F32)
                wv_ = d["gla_w_in"].ap()[j].rearrange("(k p) n -> p k n", p=128)
                for kc in range(KC):
                    fw.dma("pool", Wqk[:, kc, :], wv_[:, kc, 0:1024], writes=[("glWqk", kc)])
                    fw.dma("pool", Wv[:, kc, :], wv_[:, kc, 1024:2048], writes=[("glWv", kc)])
                fw.dma("pool", Wlr[:], wv_[:, :, 3072:3088], writes=["glWlr"])
                fw.dma("pool", Wgk2[:], d["gla_w_gk2"].ap()[j], writes=["glWgk2"])
                fw.dma("sp", nbg[:], d["gla_b_gkT"].ap()[j], writes=["glnbg"])
                fw.op("dve", lambda e: e.tensor_scalar(nbg[:], nbg[:], -1.0, None, ALU.mult), reads=["glnbg"], writes=["glnbg"])
                fw.op("dve", lambda e: e.memset(S[:], 0.0), writes=["glS"])
                fw.op("dve", lambda e: e.memset(Sb[:], 0.0), writes=["glSb"])
                fw.op("dve", lambda e: e.memset(cum[:], 0.0), writes=[("glcum", h) for h in range(NH)])
                wqk_keys = [("glWqk", kc) for kc in range(KC)]
                wv_keys = [("glWv", kc) for kc in range(KC)]
                for b in range(TPC // TB):
                    bsl = slice(b * TB, (b + 1) * TB)
                    xkeys = [("XT", t) for t in range(b * 2, b * 2 + 2)]
                    for kc in range(KC):
                        fw.op("pe", lambda e, kc=kc: e.matmul(self.psM[0:16, 0:TB], Wlr[:, kc, :], self.XT[:, kc, bsl],
                                                             start=(kc == 0), stop=(kc == KC - 1)),
                              reads=["glWlr"] + xkeys, writes=["psM"])
                    fw.op("act", lambda e: e.copy(lrT[:], self.psM[0:16, 0:TB]), reads=["psM"], writes=["gllrT"])
                    for h in range(NH):
                        hs = slice(h * 128, (h + 1) * 128)
                        fw.op("pe", lambda e, hs=hs: e.matmul(self.psA[0][:, 0:TB], Wgk2[:, hs], lrT[:], start=True, stop=True),
                              reads=["glWgk2", "gllrT"], writes=[("psA", 0)])
                        fw.op("act", lambda e, h=h: e.activation(e1[:], self.psA[0][:, 0:TB], AF.Exp, bias=nbg[:, h:h + 1], scale=-1.0),
                              reads=[("psA", 0), "glnbg"], writes=["gle1"])
                        fw.op("act", lambda e: e.activation(sp[:], e1[:], AF.Ln, bias=1.0, scale=1.0), reads=["gle1"], writes=["glsp"])
                        fw.op("dve", lambda e, h=h: e.tensor_tensor_scan(cum[:, h, 1:TB + 1], self.ones[:, 0:TB], sp[:],
                                                                        cum[:, h, 0:1], ALU.mult, ALU.add),
                              reads=["glsp", "ones", ("glcum", h)], writes=[("glcum", h)])
                        cprev = cum[:, h, 0:TB].rearrange("p (c s) -> p c s", s=64)[:, :, 0:1].broadcast_to([128, NCH, 64])
                        nb3 = nb[:].rearrange("p (c s) -> p c s", s=64)
                        fw.op("dve", lambda e, h=h, cprev=cprev: e.tensor_tensor(
                            nb3, cum[:, h, 1:TB + 1].rearrange("p (c s) -> p c s", s=64), cprev, ALU.subtract),
                            reads=[("glcum", h)], writes=["glnb"])
                        fw.op("dve", lambda e: e.tensor_tensor(dref[:].rearrange("p (c s) -> p c s", s=64), nb3,
                                                               nb3[:, :, 32:33].broadcast_to([128, NCH, 64]), ALU.subtract),
                              reads=["glnb"], writes=["gldref"])
                        fw.op("dve", lambda e: e.tensor_tensor(dlast[:].rearrange("p (c s) -> p c s", s=64), nb3,
                                                               nb3[:, :, 63:64].broadcast_to([128, NCH, 64]), ALU.subtract),
                              reads=["glnb"], writes=["gldlast"])
                        fw.op("act", lambda e: e.activation(E[0][:], dref[:], AF.Exp, scale=-1.0 / 16), reads=["gldref"], writes=[("glE", 0)])
                        fw.op("act", lambda e: e.activation(E[1][:], dref[:], AF.Exp, scale=1.0 / 16), reads=["gldref"], writes=[("glE", 1)])
                        fw.op("act", lambda e: e.activation(E[2][:], dlast[:], AF.Exp, scale=1.0 / 16), reads=["gldlast"], writes=[("glE", 2)])
                        fw.op("act", lambda e: e.activation(E[3][:], nb[:], AF.Exp, scale=-1.0 / 16), reads=["glnb"], writes=[("glE", 3)])
                        fw.op("act", lambda e, h=h: e.activation(E[4][:], cum[:, h, 1:TB + 1], AF.Exp, scale=-1.0 / 16),
                              reads=[("glcum", h)], writes=[("glE", 4)])
                        fw.op("act", lambda e, h=h: e.activation(dec[:, h, :], nb3[:, :, 63], AF.Exp, scale=-1.0 / 16),
                              reads=["glnb"], writes=[("gldec", h)])
                        fw.op("dve", lambda e, h=h: e.tensor_copy(cum[:, h, 0:1], cum[:, h, TB:TB + 1]),
                              reads=[("glcum", h)], writes=[("glcum", h)])
                        for which, col0 in ((1, h * 128), (2, 512 + h * 128)):
                            for kc in range(KC):
                                fw.op("pe", lambda e, kc=kc, which=which, col0=col0: e.matmul(
                                    self.psA[which][:, 0:TB], Wqk[:, kc, col0:col0 + 128], self.XT[:, kc, bsl],
                                    start=(kc == 0), stop=(kc == KC - 1)), reads=wqk_keys + xkeys, writes=[("psA", which)])
                        stt = lambda out, ps_, sc, Ei: (lambda e: e.scalar_tensor_tensor(out, ps_, sc, Ei, ALU.mult, ALU.mult))
                        fw.op("dve", stt(qp[:, h, :], self.psA[1][:, 0:TB], SC, E[0][:]), reads=[("psA", 1), ("glE", 0)], writes=[("glqp", h)])
                        fw.op("dve", stt(qpp[:, h, :], self.psA[1][:, 0:TB], SC, E[3][:]), reads=[("psA", 1), ("glE", 3)], writes=[("glqpp", h)])
                        fw.op("dve", stt(QET[:, h, bsl], self.psA[1][:, 0:TB], SC, E[4][:]), reads=[("psA", 1), ("glE", 4)], writes=[("QET", (b * TB) // 512)])
                        fw.op("dve", stt(kp[:, h, :], self.psA[2][:, 0:TB], 1.0, E[1][:]), reads=[("psA", 2), ("glE", 1)], writes=[("glkp", h)])
                        fw.op("dve", stt(kpp[:, h, :], self.psA[2][:, 0:TB], 1.0, E[2][:]), reads=[("psA", 2), ("glE", 2)], writes=[("glkpp", h)])
                    allh = lambda nm: [(nm, h) for h in range(NH)]
                    for c in range(TB // 64):
                        csl = slice(c * 64, (c + 1) * 64)
                        tok0 = b * TB + c * 64
                        gsl = slice(tok0, tok0 + 64)
                        for hf in range(2):
                            for kc in range(KC):
                                fw.op("pe", lambda e, kc=kc, hf=hf: e.matmul(
                                    self.psO[0:64, hf * 512:(hf + 1) * 512], self.XT[:, kc, gsl], Wv[:, kc, hf * 512:(hf + 1) * 512],
                                    start=(kc == 0), stop=(kc == KC - 1)), reads=wv_keys + xkeys, writes=["psO"])
                        fw.op("act", lambda e: e.copy(vb[:], self.psO[0:64, :]), reads=["psO"], writes=["glvb"])
                        for h in range(NH):
                            fw.op("pe", lambda e, h=h: e.matmul(self.psA[0][0:64, h * 64:(h + 1) * 64], kp[:, h, csl], qp[:, h, csl],
                                                                start=True, stop=True),
                                  reads=[("glkp", h), ("glqp", h)], writes=[("psA", 0)])
                        fw.op("dve", lambda e: e.tensor_tensor(
                            AT[:], self.psA[0][0:64, 0:256].rearrange("p (h i) -> p h i", h=NH),
                            self.triu[0:64, 0:64].unsqueeze(1).broadcast_to([64, NH, 64]), ALU.mult),
                            reads=[("psA", 0), "triu"], writes=["glAT"])
                        for h in range(NH):
                            fw.op("pe", lambda e, h=h: e.transpose(self.psT[0:64, h, :], kpp[:, h, csl], self.ident[:]),
                                  reads=[("glkpp", h), "ident"], writes=["psT"])
                        fw.op("act", lambda e: e.copy(kT[:], self.psT[0:64, 0:NH, :]), reads=["psT"], writes=["glkT"])
                        for h in range(NH):
                            ob = self.psA[2 + h // 2][0:64, (h % 2) * 256:(h % 2 + 1) * 256]
                            fw.op("pe", lambda e, h=h, ob=ob: e.matmul(ob, AT[:, h, :], vb[:, h * DV:(h + 1) * DV], start=True, stop=False),
                                  reads=["glAT", "glvb"], writes=[("psA", 2 + h // 2)])
                            fw.op("pe", lambda e, h=h, ob=ob: e.matmul(ob, qpp[:, h, csl], Sb[:, h * DV:(h + 1) * DV], start=False, stop=True),
                                  reads=[("glqpp", h), "glSb"], writes=[("psA", 2 + h // 2)])
                        for hf in range(2):
                            fw.op("act", lambda e, hf=hf: e.copy(oloc[:, hf * 512:(hf + 1) * 512], self.psA[2 + hf][0:64, :]),
                                  reads=[("psA", 2 + hf)], writes=["gloloc"])
                        fw.dma("sp", d["oscr"].ap()[tok0:tok0 + 64, :], oloc[:], reads=["gloloc"], writes=[("oscr", tok0 // 128)])
                        for h in range(NH):
                            fw.op("pe", lambda e, h=h: e.matmul(self.psO[:, h * DV:(h + 1) * DV], kT[:, h, :], vb[:, h * DV:(h + 1) * DV],
                                                                start=True, stop=True), reads=["glkT", "glvb"], writes=["psO"])
                        for h in range(NH):
                            fw.op("dve", lambda e, h=h, c=c: e.scalar_tensor_tensor(
                                S[:, h * DV:(h + 1) * DV], S[:, h * DV:(h + 1) * DV], dec[:, h, c:c + 1], self.psO[:, h * DV:(h + 1) * DV],
                                ALU.mult, ALU.add), reads=["glS", ("gldec", h), "psO"], writes=["glS"])
                        fw.op("act", lambda e: e.copy(Sb[:], S[:]), reads=["glS"], writes=["glSb"])
                fw.op("act", lambda e: e.activation(dtot[:], cum[:, :, 0], AF.Exp, scale=-1.0 / 16),
                      reads=[("glcum", h) for h in range(NH)], writes=["gldtot"])
                fw.dma("sp", d["gst_in"].ap()[:, 0:NH * DV], S[:], reads=["glS"], writes=["gst_in"])
                fw.dma("sp", d["gst_in"].ap()[:, NH * DV:NH * DV + NH], dtot[:], reads=["gldtot"], writes=["gst_in2"])
                fw.allgather(d["gst_in"].ap(), d["gst_all"].ap(), reads=["gst_in", "gst_in2"], writes=["gst_all"])
                fw.barrier()
            with contextlib.ExitStack() as es:
                sb = lambda name, shape, dt: es.enter_context(nc.sbuf_tensor("sb_%s_%d" % (name, _uid()), list(shape), dt))
                Sall = sb("glSall", [128, NCORES, NH * DV + NH], F32)
                st = sb("glstt", [128, NH * DV], F32)
                S0 = sb("glS0", [128, NH * DV], F32)
                fw.dma("sp", Sall[:], d["gst_all"].ap().rearrange("(c p) f -> p c f", p=128), reads=["gst_all"], writes=["glSall"])
                fw.op("dve", lambda e: e.memset(st[:], 0.0), writes=["glstt"])
                fw.op("dve", lambda e: e.memset(S0[:], 0.0), writes=["glS0"])
                for c in range(NCORES):
                    fw.op("dve", lambda e, c=c: e.scalar_tensor_tensor(S0[:], st[:], self.cmask[:, c:c + 1], S0[:], ALU.mult, ALU.add),
                          reads=["glstt", "cmask", "glS0"], writes=["glS0"])
                    if c < NCORES - 1:
                        for h in range(NH):
                            fw.op("dve", lambda e, c=c, h=h: e.scalar_tensor_tensor(
                                st[:, h * DV:(h + 1) * DV], st[:, h * DV:(h + 1) * DV], Sall[:, c, NH * DV + h:NH * DV + h + 1],
                                Sall[:, c, h * DV:(h + 1) * DV], ALU.mult, ALU.add), reads=["glstt", "glSall"], writes=["glstt"])
                fw.op("act", lambda e: e.copy(S0b[:], S0[:]), reads=["glS0"], writes=["S0b"])
                fw.barrier()
            self.mix_tail(li, NH, DV, QET, S0b, d["gla_w_in"].ap()[j][:, 2048:3072], d["gla_w_out"].ap()[j],
                          d["gla_norm_wt"].ap()[j:j + 1, :])


    def gdn(self, li, j):
        nc = self.nc
        fw = self.fw
        d = self.d
        NH, DK, DV, DA = 8, 128, 128, 256
        TB = 256
        NCH = TB // 64
        self.load_ln(li * 2)
        with contextlib.ExitStack() as es_all:
            sba = lambda name, shape, dt: es_all.enter_context(nc.sbuf_tensor("sb_%s_%d" % (name, _uid()), list(shape), dt))
            RTs = sba("gdRTs", [128, NH, 64], BF16)
            S0b = sba("S0b", [128, NH * DV], BF16)
            es_all.push(self._suppress)
            with contextlib.ExitStack() as es:
                sb = lambda name, shape, dt: es.enter_context(nc.sbuf_tensor("sb_%s_%d" % (name, _uid()), list(shape), dt))
                Wc = [sb("gdWc%d" % i, [128, KC, 128], BF16) for i in range(2)]
                Wba = sb("gdWba", [128, KC, 16], BF16)
                cw = sb("gdcw", [128, 24, 4], F32)
                carry = sb("gdcarry", [128, 24, 3], F32)
                hb = sb("gdhb", [128, 3 + TB], F32)
                acc = sb("gdacc", [128, TB], F32)
                qkf = sb("gdqkf", [128, TB], F32)
                sq = sb("gdsq", [128, TB], F32)
                rinv = sb("gdrinv", [128, TB], F32)
                qT = sb("gdqT", [128, NH, TB], BF16)
                kT = sb("gdkT", [128, NH, TB], BF16)
                vT = sb("gdvT", [128, NH, TB], BF16)
                dtb = sb("gddtb", [64, NH], F32)
                negA = sb("gdnegA", [64, NH], F32)
                ba = sb("gdba", [64, 16], F32)
                beta = sb("gdbeta", [64, NH], F32)
                g = sb("gdg", [64, NH], F32)
                bcum = sb("gdbcum", [64, NH], F32)
                eb = sb("gdeb", [64, NH], F32)
                ebl = sb("gdebl", [64, NH], F32)
                beb = sb("gdbeb", [64, NH], F32)
                cd = sb("gdcd", [128, NH], F32)
                gU = sb("gdgU", [64, NH, 64], F32)
                decS = sb("gddecS", [64, NH, 64], F32)
                decT = sb("gddecT", [64, NH, 64], F32)
                P = sb("gdP", [64, NH, 64], F32)
                PT = sb("gdPT", [64, NH, 64], F32)
                TT = sb("gdTT", [64, NH, 64], F32)
                QKTm = sb("gdQKTm", [64, NH, 64], BF16)
                bv = sb("gdbv", [64, NH, DV], F32)
                kbe = sb("gdkbe", [64, NH, DK], F32)
                kd = sb("gdkd", [64, NH, DK], BF16)
                u = sb("gdu", [64, NH, DV], F32)
                wT = sb("gdwT", [128, NH, 64], BF16)
                vnew = sb("gdvnew", [64, NH, DA], BF16)
                oa = sb("gdoa", [64, 4, DA], F32)
                Rb = sb("gdRb", [64, NH, DK], BF16)
                S = sb("gdS", [128, NH, DA], F32)
                Sb = sb("gdSb", [128, NH, DA], BF16)
                es.push(self._suppress)
                wi = d["gdn_w_in"].ap()[j]
                fw.dma("pool", Wba[:], wi[:, 4096:4112].rearrange("(k p) c -> p k c", p=128), writes=["gdWba"])
                fw.dma("sp", cw[:], d["gdn_cw"].ap()[j], writes=["gdcw"])
                fw.dma("sp", dtb[:], d["gdn_dt_bias"].ap()[j:j + 1, :].partition_broadcast(64), writes=["gddtb"])
                fw.dma("sp", negA[:], d["gdn_a_log"].ap()[j:j + 1, :].partition_broadcast(64), writes=["gdnegA"])
                fw.op("act", lambda e: e.activation(negA[:], negA[:], AF.Exp), reads=["gdnegA"], writes=["gdnegA"])
                fw.op("dve", lambda e: e.tensor_scalar(negA[:], negA[:], -1.0, None, ALU.mult), reads=["gdnegA"], writes=["gdnegA"])
                fw.op("dve", lambda e: e.memset(S[:], 0.0), writes=["gdS"])
                for h in range(NH):
                    fw.op("dve", lambda e, h=h: e.tensor_copy(S[:, h, DV:DA], self.identf[:]), reads=["identf", "gdS"], writes=["gdS"])
                fw.op("act", lambda e: e.copy(Sb[:], S[:]), reads=["gdS"], writes=[("gdSb", 0), ("gdSb", 1)])
                bc3 = lambda ap_, n: ap_.unsqueeze(2).broadcast_to([64, NH, n])
                step = 0
                for b in range(TPC // TB):
                    bsl = slice(b * TB, (b + 1) * TB)
                    xkeys = [("XT", t) for t in range(b * 2, b * 2 + 2)]
                    for cc in range(24):
                        ws = step % 2
                        step += 1
                        fw.dma("pool", Wc[ws][:], wi[:, cc * 128:(cc + 1) * 128].rearrange("(k p) c -> p k c", p=128),
                               writes=[("gdWc", ws)])
                        pst = self.psA[cc % 2]
                        pk = ("psA", cc % 2)
                        for kc in range(KC):
                            fw.op("pe", lambda e, kc=kc, ws=ws, pst=pst: e.matmul(pst[:, 0:TB], Wc[ws][:, kc, :], self.XT[:, kc, bsl],
                                                                                 start=(kc == 0), stop=(kc == KC - 1)),
                                  reads=[("gdWc", ws)] + xkeys, writes=[pk])
                        if b == 0:
                            for kc in range(KC):
                                fw.op("pe", lambda e, kc=kc, ws=ws: e.matmul(self.psM[:, 0:3], Wc[ws][:, kc, :], self.XTH[:, kc, 1:4],
                                                                            start=(kc == 0), stop=(kc == KC - 1)),
                                      reads=[("gdWc", ws), "XTH"], writes=["psM"])
                            fw.op("act", lambda e: e.copy(hb[:, 0:3], self.psM[:, 0:3]), reads=["psM"], writes=["gdhb"])
                        else:
                            fw.op("act", lambda e, cc=cc: e.copy(hb[:, 0:3], carry[:, cc, :]), reads=[("gdcarry", cc)], writes=["gdhb"])
                        fw.op("act", lambda e, pst=pst: e.copy(hb[:, 3:3 + TB], pst[:, 0:TB]), reads=[pk], writes=["gdhb"])
                        fw.op("act", lambda e, cc=cc: e.copy(carry[:, cc, :], hb[:, TB:TB + 3]), reads=["gdhb"], writes=[("gdcarry", cc)])
                        fw.op("dve", lambda e, cc=cc: e.tensor_scalar(acc[:], hb[:, 3:3 + TB], cw[:, cc, 3:4], None, ALU.mult),
                              reads=["gdhb", "gdcw"], writes=["gdacc"])
                        for tap in (2, 1, 0):
                            fw.op("dve", lambda e, cc=cc, tap=tap: e.scalar_tensor_tensor(
                                acc[:], hb[:, tap:tap + TB], cw[:, cc, tap:tap + 1], acc[:], ALU.mult, ALU.add),
                                reads=["gdhb", "gdcw", "gdacc"], writes=["gdacc"])
                        h = cc % 8
                        if cc >= 16:
                            fw.op("act", lambda e, h=h: e.activation(vT[:, h, :], acc[:], AF.Silu), reads=["gdacc"], writes=[("gdvT", h)])
                        else:
                            fw.op("act", lambda e: e.activation(qkf[:], acc[:], AF.Silu), reads=["gdacc"], writes=["gdqkf"])
                            fw.op("dve", lambda e: e.tensor_tensor(sq[:], qkf[:], qkf[:], ALU.mult), reads=["gdqkf"], writes=["gdsq"])
                            fw.op("pe", lambda e: e.matmul(self.psA[2][:, 0:TB], self.onesf[:], sq[:], start=True, stop=True),
                                  reads=["onesf", "gdsq"], writes=[("psA", 2)])
                            fw.op("act", lambda e: e.activation(rinv[:], self.psA[2][:, 0:TB], AF.Sqrt, bias=RMS_EPS, scale=1.0),
                                  reads=[("psA", 2)], writes=["gdrinv"])
                            fw.op("dve", lambda e: e.reciprocal(rinv[:], rinv[:]), reads=["gdrinv"], writes=["gdrinv"])
                            dst, key, sc = (qT, "gdqT", DK ** -0.5) if cc < 8 else (kT, "gdkT", 1.0)
                            fw.op("dve", lambda e, dst=dst, h=h, sc=sc: e.scalar_tensor_tensor(dst[:, h, :], qkf[:], sc, rinv[:], ALU.mult, ALU.mult),
                                  reads=["gdqkf", "gdrinv"], writes=[(key, h)])
                    allh = lambda nm: [(nm, h) for h in range(NH)]
                    self.chk(1)
                    for c in range(NCH):
                        csl = slice(c * 64, (c + 1) * 64)
                        tok0 = b * TB + c * 64
                        gsl = slice(tok0, tok0 + 64)
                        for kc in range(KC):
                            fw.op("pe", lambda e, kc=kc: e.matmul(self.psM[0:64, 0:16], self.XT[:, kc, gsl], Wba[:, kc, :],
                                                                 start=(kc == 0), stop=(kc == KC - 1)),
                                  reads=["gdWba"] + xkeys, writes=["psM"])
                        fw.op("act", lambda e: e.copy(ba[:], self.psM[0:64, 0:16]), reads=["psM"], writes=["gdba"])
                        fw.op("act", lambda e: e.activation(beta[:], ba[:, 0:8], AF.Exp, scale=-1.0), reads=["gdba"], writes=["gdbeta"])
                        fw.op("dve", lambda e: e.tensor_scalar(beta[:], beta[:], 1.0, None, ALU.add), reads=["gdbeta"], writes=["gdbeta"])
                        fw.op("dve", lambda e: e.reciprocal(beta[:], beta[:]), reads=["gdbeta"], writes=["gdbeta"])
                        fw.op("dve", lambda e: e.tensor_tensor(g[:], ba[:, 8:16], dtb[:], ALU.add), reads=["gdba", "gddtb"], writes=["gdg"])
                        fw.op("act", lambda e: e.activation(g[:], g[:], AF.Exp), reads=["gdg"], writes=["gdg"])
                        fw.op("act", lambda e: e.activation(g[:], g[:], AF.Ln, bias=1.0, scale=1.0), reads=["gdg"], writes=["gdg"])
                        fw.op("dve", lambda e: e.tensor_tensor(g[:], g[:], negA[:], ALU.mult), reads=["gdg", "gdnegA"], writes=["gdg"])
                        fw.op("pe", lambda e: e.matmul(self.psM[0:64, 16:24], self.triu[0:64, 0:64], g[:], start=True, stop=True),
                              reads=["triu", "gdg"], writes=["psM"])
                        fw.op("pe", lambda e: e.matmul(self.psM[:, 24:32], self.onesf[0:64, :], g[:], start=True, stop=True),
                              reads=["onesf", "gdg"], writes=["psM"])
                        fw.op("act", lambda e: e.copy(bcum[:], self.psM[0:64, 16:24]), reads=["psM"], writes=["gdbcum"])
                        fw.op("act", lambda e: e.activation(eb[:], self.psM[0:64, 16:24], AF.Exp), reads=["psM"], writes=["gdeb"])
                        fw.op("act", lambda e: e.activation(cd[:], self.psM[:, 24:32], AF.Exp), reads=["psM"], writes=["gdcd"])
                        fw.op("dve", lambda e: e.tensor_tensor(ebl[:], self.psM[0:64, 24:32], bcum[:], ALU.subtract),
                              reads=["psM", "gdbcum"], writes=["gdebl"])
                        fw.op("act", lambda e: e.activation(ebl[:], ebl[:], AF.Exp), reads=["gdebl"], writes=["gdebl"])
                        fw.op("dve", lambda e: e.tensor_tensor(beb[:], beta[:], eb[:], ALU.mult), reads=["gdbeta", "gdeb"], writes=["gdbeb"])
                        for h in range(NH):
                            fw.op("dve", lambda e, h=h: e.tensor_scalar(gU[:, h, :], self.triu[0:64, 0:64], g[:, h:h + 1], None, ALU.mult),
                                  reads=["triu", "gdg"], writes=["gdgU"])
                        for h in range(NH):
                            fw.op("pe", lambda e, h=h: e.matmul(self.psA[0][0:64, h * 64:(h + 1) * 64], gU[:, h, :], self.trilS[0:64, 0:64],
                                                                start=True, stop=True), reads=["gdgU", "trilS"], writes=[("psA", 0)])
                            fw.op("pe", lambda e, h=h: e.matmul(self.psA[1][0:64, h * 64:(h + 1) * 64], self.trilS[0:64, 0:64], gU[:, h, :],
                                                                start=True, stop=True), reads=["gdgU", "trilS"], writes=[("psA", 1)])
                        v3 = lambda ps_: ps_[0:64, :].rearrange("p (h i) -> p h i", h=NH)
                        fw.op("act", lambda e: e.activation(decS[:], v3(self.psA[0]), AF.Exp), reads=[("psA", 0)], writes=["gddecS"])
                        fw.op("act", lambda e: e.activation(decT[:], v3(self.psA[1]), AF.Exp), reads=[("psA", 1)], writes=["gddecT"])
                        fw.op("dve", lambda e: e.tensor_tensor(decS[:], decS[:], self.trilS[0:64, 0:64].unsqueeze(1).broadcast_to([64, NH, 64]), ALU.mult),
                              reads=["gddecS", "trilS"], writes=["gddecS"])
                        fw.op("dve", lambda e: e.tensor_tensor(decS[:], decS[:], bc3(beta[:], 64), ALU.mult),
                              reads=["gddecS", "gdbeta"], writes=["gddecS"])
                        fw.op("dve", lambda e: e.tensor_tensor(decT[:], decT[:], self.triu[0:64, 0:64].unsqueeze(1).broadcast_to([64, NH, 64]), ALU.mult),
                              reads=["gddecT", "triu"], writes=["gddecT"])
                        for h in range(NH):
                            fw.op("pe", lambda e, h=h: e.matmul(self.psA[2][0:64, h * 64:(h + 1) * 64], kT[:, h, csl], kT[:, h, csl],
                                                                start=True, stop=True), reads=[("gdkT", h)], writes=[("psA", 2)])
                            fw.op("pe", lambda e, h=h: e.matmul(self.psA[3][0:64, h * 64:(h + 1) * 64], kT[:, h, csl], qT[:, h, csl],
                                                                start=True, stop=True), reads=[("gdkT", h), ("gdqT", h)], writes=[("psA", 3)])
                        fw.op("dve", lambda e: e.scalar_tensor_tensor(P[:], v3(self.psA[2]), -1.0, decS[:], ALU.mult, ALU.mult),
                              reads=[("psA", 2), "gddecS"], writes=["gdP"])
                        fw.op("dve", lambda e: e.tensor_tensor(QKTm[:], v3(self.psA[3]), decT[:], ALU.mult),
                              reads=[("psA", 3), "gddecT"], writes=["gdQKTm"])
                        for h in range(NH):
                            fw.op("pe", lambda e, h=h: e.transpose(self.psA[0][0:64, h * 64:(h + 1) * 64], P[:, h, :], self.identf[0:64, 0:64]),
                                  reads=["gdP", "identf"], writes=[("psA", 0)])
                        fw.op("act", lambda e: e.copy(PT[:], v3(self.psA[0])), reads=[("psA", 0)], writes=["gdPT"])
                        self.chk(2)
                        fw.op("dve", lambda e: e.tensor_tensor(TT[:], PT[:], self.identf[0:64, 0:64].unsqueeze(1).broadcast_to([64, NH, 64]), ALU.add),
                              reads=["gdPT", "identf"], writes=["gdTT"])
                        for lvl in range(5):
                            for h in range(NH):
                                fw.op("pe", lambda e, h=h: e.matmul(self.psA[0][0:64, h * 64:(h + 1) * 64], PT[:, h, :], P[:, h, :],
                                                                    start=True, stop=True), reads=["gdP", "gdPT"], writes=[("psA", 0)])
                                fw.op("pe", lambda e, h=h: e.matmul(self.psA[1][0:64, h * 64:(h + 1) * 64], P[:, h, :], PT[:, h, :],
                                                                    start=True, stop=True), reads=["gdP", "gdPT"], writes=[("psA", 1)])
                            fw.op("act", lambda e: e.copy(P[:], v3(self.psA[0])), reads=[("psA", 0)], writes=["gdP"])
                            fw.op("act", lambda e: e.copy(PT[:], v3(self.psA[1])), reads=[("psA", 1)], writes=["gdPT"])
                            for h in range(NH):
                                fw.op("pe", lambda e, h=h: e.matmul(self.psA[2][0:64, h * 64:(h + 1) * 64], P[:, h, :], TT[:, h, :],
                                                                    start=True, stop=True), reads=["gdP", "gdTT"], writes=[("psA", 2)])
                            fw.op("dve", lambda e: e.tensor_tensor(TT[:], TT[:], v3(self.psA[2]), ALU.add), reads=["gdTT", ("psA", 2)], writes=["gdTT"])
                        self.chk(3)
                        for h in range(NH):
                            fw.op("pe", lambda e, h=h: e.transpose(self.psT[0:64, h, :], vT[:, h, csl], self.ident[:]),
                                  reads=[("gdvT", h), "ident"], writes=["psT"])
                        fw.op("dve", lambda e: e.tensor_tensor(bv[:], self.psT[0:64, :, :], bc3(beta[:], DV), ALU.mult),
                              reads=["psT", "gdbeta"], writes=["gdbv"])
                        for h in range(NH):
                            fw.op("pe", lambda e, h=h: e.transpose(self.psT[0:64, h, :], kT[:, h, csl], self.ident[:]),
                                  reads=[("gdkT", h), "ident"], writes=["psT"])
                        fw.op("dve", lambda e: e.tensor_tensor(kbe[:], self.psT[0:64, :, :], bc3(beb[:], DK), ALU.mult),
                              reads=["psT", "gdbeb"], writes=["gdkbe"])
                        fw.op("dve", lambda e: e.tensor_tensor(kd[:], self.psT[0:64, :, :], bc3(ebl[:], DK), ALU.mult),
                              reads=["psT", "gdebl"], writes=["gdkd"])
                        for h in range(NH):
                            fw.op("pe", lambda e, h=h: e.matmul(self.psO[0:64, h * DV:(h + 1) * DV], TT[:, h, :], bv[:, h, :], start=True, stop=True),
                                  reads=["gdTT", "gdbv"], writes=["psO"])
                            fw.op("pe", lambda e, h=h: e.matmul(self.psA[3][:, h * 64:(h + 1) * 64], kbe[:, h, :], TT[:, h, :], start=True, stop=True),
                                  reads=["gdTT", "gdkbe"], writes=[("psA", 3)])
                        fw.op("act", lambda e: e.copy(u[:], self.psO[0:64, :].rearrange("p (h v) -> p h v", h=NH)), reads=["psO"], writes=["gdu"])
                        fw.op("act", lambda e: e.copy(wT[:], self.psA[3][:].rearrange("p (h s) -> p h s", h=NH)), reads=[("psA", 3)], writes=["gdwT"])
                        self.chk(4)
                        for hh in range(2):
                            hs = range(hh * 4, hh * 4 + 4)
                            for i, h in enumerate(hs):
                                fw.op("pe", lambda e, h=h, i=i: e.matmul(self.psO[0:64, i * DA:(i + 1) * DA], wT[:, h, :], Sb[:, h, :], start=True, stop=True),
                                      reads=["gdwT", ("gdSb", hh)], writes=["psO"])
                            wS = self.psO[0:64, :].rearrange("p (h a) -> p h a", h=4)
                            fw.op("dve", lambda e, hh=hh, wS=wS: e.tensor_tensor(vnew[:, hh * 4:hh * 4 + 4, 0:DV], u[:, hh * 4:hh * 4 + 4, :], wS[:, :, 0:DV], ALU.subtract),
                                  reads=["gdu", "psO"], writes=[("gdvnew", hh)])
                            fw.op("dve", lambda e, hh=hh, wS=wS: e.tensor_scalar(vnew[:, hh * 4:hh * 4 + 4, DV:DA], wS[:, :, DV:DA], -1.0, None, ALU.mult),
                                  reads=["psO"], writes=[("gdvnew", hh)])
                            for i, h in enumerate(hs):
                                p1 = self.psA[i // 2][0:64, (i % 2) * DA:(i % 2 + 1) * DA]
                                p2 = self.psA[2 + i // 2][0:64, (i % 2) * DA:(i % 2 + 1) * DA]
                                fw.op("pe", lambda e, h=h, p1=p1: e.matmul(p1, qT[:, h, csl], Sb[:, h, :], start=True, stop=True),
                                      reads=[("gdqT", h), ("gdSb", hh)], writes=[("psA", i // 2)])
                                fw.op("pe", lambda e, h=h, p2=p2: e.matmul(p2, QKTm[:, h, :], vnew[:, h, :], start=True, stop=True),
                                      reads=["gdQKTm", ("gdvnew", hh)], writes=[("psA", 2 + i // 2)])
                            for i, h in enumerate(hs):
                                p1 = self.psA[i // 2][0:64, (i % 2) * DA:(i % 2 + 1) * DA]
                                p2 = self.psA[2 + i // 2][0:64, (i % 2) * DA:(i % 2 + 1) * DA]
                                fw.op("act", lambda e, i=i, p2=p2: e.copy(oa[:, i, :], p2), reads=[("psA", 2 + i // 2)], writes=[("gdoa", i)])
                                fw.op("dve", lambda e, i=i, h=h, p1=p1: e.scalar_tensor_tensor(oa[:, i, :], p1, eb[:, h:h + 1], oa[:, i, :], ALU.mult, ALU.add),
                                      reads=[("psA", i // 2), "gdeb", ("gdoa", i)], writes=[("gdoa", i)])
                            fw.dma("sp", d["oscr"].ap()[tok0:tok0 + 64, hh * 512:(hh + 1) * 512].rearrange("p (h v) -> p h v", h=4), oa[:, :, 0:DV],
                                   reads=[("gdoa", i) for i in range(4)], writes=[("oscr", tok0 // 128, hh)])
                            fw.op("act", lambda e, hh=hh: e.copy(Rb[:, hh * 4:hh * 4 + 4, :], oa[:, :, DV:DA]),
                                  reads=[("gdoa", i) for i in range(4)], writes=[("gdRb", hh)])
                            for i, h in enumerate(hs):
                                fw.op("pe", lambda e, h=h, i=i: e.matmul(self.psO[:, i * DA:(i + 1) * DA], kd[:, h, :], vnew[:, h, :], start=True, stop=True),
                                      reads=["gdkd", ("gdvnew", hh)], writes=["psO"])
                            for i, h in enumerate(hs):
                                fw.op("dve", lambda e, h=h, i=i: e.scalar_tensor_tensor(S[:, h, :], S[:, h, :], cd[:, h:h + 1], self.psO[:, i * DA:(i + 1) * DA],
                                                                                       ALU.mult, ALU.add),
                                      reads=["gdS", "gdcd", "psO"], writes=["gdS"])
                            fw.op("act", lambda e, hh=hh: e.copy(Sb[:, hh * 4:hh * 4 + 4, :], S[:, hh * 4:hh * 4 + 4, :]),
                                  reads=["gdS"], writes=[("gdSb", hh)])
                        for h in range(NH):
                            fw.op("pe", lambda e, h=h: e.transpose(self.psT[:, h, 0:64], Rb[:, h, :], self.ident[0:64, 0:64]),
                                  reads=[("gdRb", 0), ("gdRb", 1), "ident"], writes=["psT"])
                        fw.op("act", lambda e: e.copy(RTs[:], self.psT[:, :, 0:64]), reads=["psT"], writes=["gdRTs"])
                        fw.dma("sp", d["rtscr"].ap()[:, :, gsl], RTs[:], reads=["gdRTs"], writes=[("rtscr", tok0 // 128)])
                    if b == 0:
                        self.chk(5)
                self.chk(6)
                fw.dma("sp", d["gd_in"].ap(), S[:].rearrange("p h a -> p (h a)"), reads=["gdS"], writes=["gd_in"])
                fw.allgather(d["gd_in"].ap(), d["gd_all"].ap(), reads=["gd_in"], writes=["gd_all"])
                fw.barrier()
            if self.stopped:
                fw.barrier()
                return
            with contextlib.ExitStack() as es:
                sb = lambda name, shape, dt: es.enter_context(nc.sbuf_tensor("sb_%s_%d" % (name, _uid()), list(shape), dt))
                Sall = sb("gdSall", [128, NCORES, DA], F32)
                st = sb("gdstt", [128, DV], F32)
                S0 = sb("gdS0", [128, DV], F32)
                PTs = sb("gdPTs", [128, 128], F32)
                gv = d["gd_all"].ap().rearrange("(c p) (h a) -> p c h a", p=128, h=NH)
                for h in range(NH):
                    fw.dma("sp", Sall[:], gv[:, :, h, :], reads=["gd_all"], writes=["gdSall"])
                    fw.op("dve", lambda e: e.memset(st[:], 0.0), writes=["gdstt"])
                    fw.op("dve", lambda e: e.memset(S0[:], 0.0), writes=["gdS0"])
                    for c in range(NCORES):
                        fw.op("dve", lambda e, c=c: e.scalar_tensor_tensor(S0[:], st[:], self.cmask[:, c:c + 1], S0[:], ALU.mult, ALU.add),
                              reads=["gdstt", "cmask", "gdS0"], writes=["gdS0"])
                        if c < NCORES - 1:
                            fw.op("pe", lambda e, c=c: e.transpose(self.psM[:, 0:128], Sall[:, c, DV:DA], self.identf[:]),
                                  reads=["gdSall", "identf"], writes=["psM"])
                            fw.op("act", lambda e: e.copy(PTs[:], self.psM[:, 0:128]), reads=["psM"], writes=["gdPTs"])
                            fw.op("pe", lambda e: e.matmul(self.psA[0][:, 0:128], PTs[:], st[:], start=True, stop=True),
                                  reads=["gdPTs", "gdstt"], writes=[("psA", 0)])
                            fw.op("dve", lambda e, c=c: e.tensor_tensor(st[:], self.psA[0][:, 0:128], Sall[:, c, 0:DV], ALU.add),
                                  reads=[("psA", 0), "gdSall"], writes=["gdstt"])
                    fw.op("act", lambda e, h=h: e.copy(S0b[:, h * DV:(h + 1) * DV], S0[:]), reads=["gdS0"], writes=["S0b"])
                fw.barrier()
            self.chk(7)
            self.mix_tail(li, NH, DV, None, S0b, d["gdn_w_in"].ap()[j][:, 3072:4096], d["gdn_w_out"].ap()[j],
                          d["gdn_norm_wt"].ap()[j:j + 1, :])

    def mix_tail(self, li, NH, DV, QET, S0b, wg_ap, wo_ap, nw_ap):
        nc = self.nc
        fw = self.fw
        d = self.d
        with contextlib.ExitStack() as es:
            sb = lambda name, shape, dt: es.enter_context(nc.sbuf_tensor("sb_%s_%d" % (name, _uid()), list(shape), dt))
            Wg = sb("mtWg", [128, KC, D], BF16)
            WOm = sb("mtWO", [128, KC, D], BF16)
            nw = sb("mtnw", [128, D], F32)
            ol = sb("mtol", [128, D], F32)
            sg = sb("mtsg", [128, D], F32)
            og = sb("mtog", [128, D], BF16)
            st = sb("mtst", [128, NH, 6], F32)
            mvh = sb("mtmv", [128, NH, 2], F32)
            ms = sb("mtms", [128, NH], F32)
            self.load_wout(Wg, wg_ap, "mtWg")
            self.load_wout(WOm, wo_ap, "mtWO")
            fw.dma("sp", nw[:], nw_ap.partition_broadcast(128), writes=["mtnw"])
            QK = [("QET", b) for b in range(4)]
            if QET is None:
                qst = sb("mtqst", [128, NH, 128], BF16)
            for t in range(NT):
                tsl = slice(t * 128, (t + 1) * 128)
                fw.dma("sp", ol[:], d["oscr"].ap()[t * 128:(t + 1) * 128, :], reads=[("oscr", t), ("oscr", t, 0), ("oscr", t, 1)], writes=["mtol"])
                if QET is None:
                    fw.dma("sp", qst[:], d["rtscr"].ap()[:, :, tsl], reads=[("rtscr", t)], writes=["mtqst"])
                for h in range(NH):
                    col = h * DV
                    lhs = QET[:, h, tsl] if QET is not None else qst[:, h, :]
                    fw.op("pe", lambda e, h=h, col=col, lhs=lhs: e.matmul(self.psA[col // 512][:, col % 512:col % 512 + DV], lhs,
                                                                          S0b[:, col:col + DV], start=True, stop=True),
                          reads=(QK if QET is not None else ["mtqst"]) + ["S0b"], writes=[("psA", col // 512)])
                for hf in range(2):
                    fw.op("dve", lambda e, hf=hf: e.tensor_tensor(ol[:, hf * 512:(hf + 1) * 512], ol[:, hf * 512:(hf + 1) * 512],
                                                                  self.psA[hf][:], ALU.add),
                          reads=["mtol", ("psA", hf)], writes=["mtol"])
                for hf in range(2):
                    for kc in range(KC):
                        fw.op("pe", lambda e, kc=kc, hf=hf: e.matmul(self.psA[2 + hf][:], self.XT[:, kc, tsl], Wg[:, kc, hf * 512:(hf + 1) * 512],
                                                                     start=(kc == 0), stop=(kc == KC - 1)),
                              reads=[("XT", t), "mtWg"] + [("mtWg", k) for k in range(1, KC)], writes=[("psA", 2 + hf)])
                    fw.op("act", lambda e, hf=hf: e.activation(sg[:, hf * 512:(hf + 1) * 512], self.psA[2 + hf][:], AF.Silu),
                          reads=[("psA", 2 + hf)], writes=["mtsg"])
                for h in range(NH):
                    fw.op("dve", lambda e, h=h: e.bn_stats(st[:, h, :], ol[:, h * DV:(h + 1) * DV]), reads=["mtol"], writes=["mtst"])
                    fw.op("dve", lambda e, h=h: e.bn_aggr(mvh[:, h, :], st[:, h, :]), reads=["mtst"], writes=["mtmv"])
                fw.op("dve", lambda e: e.tensor_tensor(ms[:], mvh[:, :, 0], mvh[:, :, 0], ALU.mult), reads=["mtmv"], writes=["mtms"])
                fw.op("dve", lambda e: e.tensor_tensor(ms[:], ms[:], mvh[:, :, 1], ALU.add), reads=["mtms", "mtmv"], writes=["mtms"])
                fw.op("act", lambda e: e.activation(ms[:], ms[:], AF.Sqrt, bias=RMS_EPS, scale=1.0), reads=["mtms"], writes=["mtms"])
                fw.op("dve", lambda e: e.reciprocal(ms[:], ms[:]), reads=["mtms"], writes=["mtms"])
                fw.op("dve", lambda e: e.tensor_tensor(ol[:].rearrange("p (h v) -> p h v", h=NH), ol[:].rearrange("p (h v) -> p h v", h=NH),
                                                       ms[:].unsqueeze(2).broadcast_to([128, NH, DV]), ALU.mult),
                      reads=["mtol", "mtms"], writes=["mtol"])
                fw.op("pool", lambda e: e.tensor_tensor(ol[:], ol[:], nw[:], ALU.mult), reads=["mtol", "mtnw"], writes=["mtol"])
                fw.op("dve", lambda e: e.tensor_tensor(og[:], ol[:], sg[:], ALU.mult), reads=["mtol", "mtsg"], writes=["mtog"])
                self.out_proj_ln(t, og, ["mtog"], WOm, "mtWO", li * 2)
            fw.barrier()

    def load_ln(self, li):
        fw = self.fw
        fw.dma("sp", self.lng[:], self.d["ln_g"].ap()[li:li + 1, :].partition_broadcast(128), writes=["lng"])
        fw.dma("sp", self.lnb[:], self.d["ln_b"].ap()[li:li + 1, :].partition_broadcast(128), writes=["lnb"])

    def ffn(self, layer):
        nc = self.nc
        fw = self.fw
        d = self.d
        TB = 512
        NTB = TPC // TB
        self.load_ln(layer * 2 + 1)
        with contextlib.ExitStack() as es:
            sb = lambda name, shape, dt: es.enter_context(nc.sbuf_tensor("sb_%s_%d" % (name, _uid()), list(shape), dt))
            WO = sb("WO", [128, NFC, D], BF16)
            Wb = [sb("Wb%d" % i, [128, KC, 256], BF16) for i in range(2)]
            cw = sb("cw", [128, 2 * NFC, 3], F32)
            carry = sb("carry", [128, 2 * NFC, 2], F32)
            hb = [sb("hb%d" % i, [128, 2 + TB], F32) for i in range(2)]
            acc = [sb("acc%d" % i, [128, TB], F32) for i in range(2)]
            gg = sb("gg", [128, TB], F32)
            G = sb("G", [128, NFC, TB], BF16)

            fw.dma("sp", cw[:], d["ffn_cw"].ap()[layer], writes=["cw"])
            wo_v = d["ffn_w_out"].ap()[layer].rearrange("(j p) n -> p j n", p=128)
            for j in range(NFC):
                fw.dma("pool", WO[:, j, :], wo_v[:, j, :], writes=[("WO", j)])
            wi = d["ffn_w_in"].ap()[layer]
            step = 0
            for tb in range(NTB):
                tsl = slice(tb * TB, (tb + 1) * TB)
                for j in range(NFC):
                    s = step % 2
                    step += 1
                    for half, col0 in enumerate((j * 128, DFF + j * 128)):
                        fw.dma("pool", Wb[s][:, :, half * 128:(half + 1) * 128],
                               wi[:, col0:col0 + 128].rearrange("(k p) c -> p k c", p=128),
                               writes=[("Wb", s)] if half == 0 else [("Wb2", s)])
                    for half in range(2):
                        pst = self.psA[half * 2 + (j % 2)]
                        pkey = ("psA", half * 2 + (j % 2))
                        for kc in range(KC):
                            fw.op("pe", lambda e, kc=kc, half=half, pst=pst, s=s: e.matmul(
                                pst[:], Wb[s][:, kc, half * 128:(half + 1) * 128], self.XT[:, kc, tsl],
                                start=(kc == 0), stop=(kc == KC - 1)),
                                reads=[("Wb", s), ("Wb2", s)] + [("XT", t) for t in range(tb * 4, tb * 4 + 4)], writes=[pkey])
                        ch = half * NFC + j
                        if tb == 0:
                            for kc in range(KC):
                                fw.op("pe", lambda e, kc=kc, half=half, s=s: e.matmul(
                                    self.psM[:, 0:2], Wb[s][:, kc, half * 128:(half + 1) * 128], self.XTH[:, kc, 2:4],
                                    start=(kc == 0), stop=(kc == KC - 1)),
                                    reads=[("Wb", s), ("Wb2", s), "XTH"], writes=["psM"])
                            fw.op("act", lambda e, half=half: e.copy(hb[half][:, 0:2], self.psM[:, 0:2]),
                                  reads=["psM"], writes=[("hb", half)])
                        else:
                            fw.op("act", lambda e, half=half, ch=ch: e.copy(hb[half][:, 0:2], carry[:, ch, :]),
                                  reads=[("carry", ch)], writes=[("hb", half)])
                        fw.op("act", lambda e, half=half, pst=pst: e.copy(hb[half][:, 2:2 + TB], pst[:]),
                              reads=[pkey], writes=[("hb", half)])
                        fw.op("act", lambda e, half=half, ch=ch: e.copy(carry[:, ch, :], hb[half][:, TB:TB + 2]),
                              reads=[("hb", half)], writes=[("carry", ch)])
                        a = acc[half]
                        fw.op("dve", lambda e, half=half, ch=ch, a=a: e.tensor_scalar(
                            a[:], hb[half][:, 2:2 + TB], cw[:, ch, 2:3], None, ALU.mult),
                            reads=[("hb", half), "cw"], writes=[("acc", half)])
                        fw.op("dve", lambda e, half=half, ch=ch, a=a: e.scalar_tensor_tensor(
                            a[:], hb[half][:, 1:1 + TB], cw[:, ch, 1:2], a[:], ALU.mult, ALU.add),
                            reads=[("hb", half), "cw", ("acc", half)], writes=[("acc", half)])
                        fw.op("dve", lambda e, half=half, ch=ch, a=a: e.scalar_tensor_tensor(
                            a[:], hb[half][:, 0:TB], cw[:, ch, 0:1], a[:], ALU.mult, ALU.add),
                            reads=[("hb", half), "cw", ("acc", half)], writes=[("acc", half)])
                    fw.op("act", lambda e: e.activation(gg[:], acc[0][:], AF.Gelu),
                          reads=[("acc", 0)], writes=["gg"])
                    fw.op("dve", lambda e, j=j: e.tensor_tensor(G[:, j, :], gg[:], acc[1][:], ALU.mult),
                          reads=["gg", ("acc", 1)], writes=[("G", j)])
                for tt in range(TB // 128):
                    t = tb * (TB // 128) + tt
                    for j in range(NFC):
                        for hf in range(2):
                            fw.op("pe", lambda e, j=j, hf=hf, tt=tt: e.matmul(
                                self.psO[:, hf * 512:(hf + 1) * 512], G[:, j, tt * 128:(tt + 1) * 128],
                                WO[:, j, hf * 512:(hf + 1) * 512], start=(j == 0), stop=(j == NFC - 1)),
                                reads=[("G", j), ("WO", j)], writes=["psO"])
                    self.layer_norm_tile(t, self.psO[:], layer * 2 + 1, ["psO"])
                    self.make_xt(t)
            fw.barrier()


_CACHE = {}


def _host_consts(core):
    ident = np.eye(128, dtype=np.float32)
    hmask = np.zeros((128, NCORES, 32), np.float32)
    if core > 0:
        hmask[:, core - 1, :] = 1.0
    triu = np.triu(np.ones((128, 128), np.float32))
    cmask = np.zeros((128, NCORES), np.float32)
    cmask[:, core] = 1.0
    trilS = np.tril(np.ones((128, 128), np.float32), -1)
    return dict(ident=ident, hmask=hmask, triu=triu, cmask=cmask, trilS=trilS)


def _prep_inputs(inputs, names):
    x = np.ascontiguousarray(inputs["x"]).reshape(SEQ, D)
    shared = {}
    if "ffn_w_in" in names:
        shared["ffn_w_in"] = np.ascontiguousarray(inputs["ffn_w_in"])
        shared["ffn_w_out"] = np.ascontiguousarray(inputs["ffn_w_out"])
        cw = np.asarray(inputs["ffn_conv_w"])
        shared["ffn_cw"] = np.ascontiguousarray(cw.reshape(DEPTH, 3, 2 * NFC, 128).transpose(0, 3, 2, 1))
        shared["ln_g"] = np.ascontiguousarray(inputs["ln_g"]).reshape(DEPTH * 2, D)
        shared["ln_b"] = np.ascontiguousarray(inputs["ln_b"]).reshape(DEPTH * 2, D)
    if "gla_w_in" in names:
        shared["gla_w_in"] = np.ascontiguousarray(inputs["gla_w_in"])
        shared["gla_w_gk2"] = np.ascontiguousarray(inputs["gla_w_gk2"])
        shared["gla_b_gkT"] = np.ascontiguousarray(np.asarray(inputs["gla_b_gk"]).reshape(2, 4, 128).transpose(0, 2, 1))
        shared["gla_norm_wt"] = np.ascontiguousarray(np.tile(np.asarray(inputs["gla_norm_w"]), (1, 4)))
        shared["gla_w_out"] = np.ascontiguousarray(inputs["gla_w_out"])
    if "gdn_w_in" in names:
        shared["gdn_w_in"] = np.ascontiguousarray(inputs["gdn_w_in"])
        cwg = np.asarray(inputs["gdn_conv_w"])
        shared["gdn_cw"] = np.ascontiguousarray(cwg.reshape(1, 4, 24, 128).transpose(0, 3, 2, 1))
        shared["gdn_a_log"] = np.ascontiguousarray(inputs["gdn_a_log"])
        shared["gdn_dt_bias"] = np.ascontiguousarray(inputs["gdn_dt_bias"])
        shared["gdn_norm_wt"] = np.ascontiguousarray(np.tile(np.asarray(inputs["gdn_norm_w"]), (1, 8)))
        shared["gdn_w_out"] = np.ascontiguousarray(inputs["gdn_w_out"])
    if "sg_w_in" in names:
        shared["sg_w_in"] = np.ascontiguousarray(inputs["sg_w_in"])
        shared["sg_w_out"] = np.ascontiguousarray(inputs["sg_w_out"])
        shared["sg_wspT"] = np.ascontiguousarray(np.asarray(inputs["sg_w_sp"]).transpose(0, 3, 1, 2))
        shared["sg_bspT"] = np.ascontiguousarray(np.asarray(inputs["sg_b_sp"]).transpose(0, 2, 1))
        shared["sg_ln_g"] = np.ascontiguousarray(inputs["sg_ln_g"])
        shared["sg_ln_b"] = np.ascontiguousarray(inputs["sg_ln_b"])
    maps = []
    for c in range(NCORES):
        m = dict(shared)
        m["x"] = np.ascontiguousarray(x[c * TPC:(c + 1) * TPC])
        m.update(_host_consts(c))
        maps.append({k: v for k, v in m.items() if k in names})
    return maps


def run(inputs, stages):
    import time
    t0 = time.time()
    b = Builder(stages)
    nc = b.build()
    print("build s", time.time() - t0, flush=True)
    names = set(b.din.keys())
    maps = _prep_inputs(inputs, names)
    t0 = time.time()
    import os
    tr = bool(os.environ.get("KTRACE"))
    res = run_bass_kernel_spmd(nc, maps, core_ids=list(range(NCORES)), **({"trace": True} if tr else {}))
    print("run s", time.time() - t0, "exec_time_ns", getattr(res, "exec_time_ns", None), flush=True)
    out = np.concatenate([np.asarray(r["out"]) for r in res.results], axis=0)
    return out.reshape(1, SEQ, D).astype(np.float32)


def kernel(**inputs):
    return run(inputs, ("full",))
```

```python
import contextlib
import numpy as np
import concourse.bass as bass
import concourse.mybir as mybir
from concourse.bass_utils import run_bass_kernel_spmd

F32 = mybir.dt.float32
BF16 = mybir.dt.bfloat16
AF = mybir.ActivationFunctionType
ALU = mybir.AluOpType
AX = mybir.AxisListType

NCORES = 8
SEQ = 16384
D = 1024
TPC = SEQ // NCORES
NT = TPC // 128
KC = D // 128
DFF = 2816
NFC = DFF // 128
DEPTH = 4
ALPHA = (2 * DEPTH) ** 0.25
LN_EPS = 1e-5
RMS_EPS = 1e-6


class FW:
    CE = ("pe", "dve", "act", "pool", "sp")
    NL = 6

    def __init__(self, nc, es):
        self.nc = nc
        self.eng = dict(pe=nc.tensor, dve=nc.vector, act=nc.scalar, pool=nc.gpsimd, sp=nc.sync)
        self.sem = {}
        self.cnt = {}
        for e in self.CE:
            self.sem[e] = es.enter_context(nc.semaphore("s_" + e))
            self.cnt[e] = 0
        self.lanes = {}
        for q in ("sp", "pool", "act"):
            self.lanes[q] = []
            for i in range(self.NL):
                r = "dma_%s_%d" % (q, i)
                self.sem[r] = es.enter_context(nc.semaphore("s_" + r))
                self.cnt[r] = 0
                self.lanes[q].append(r)
        self.lane_next = {q: 0 for q in self.lanes}
        self.sem["cc"] = es.enter_context(nc.semaphore("s_cc"))
        self.cnt["cc"] = 0
        self.clock = {e: {} for e in self.CE}
        self.W = {}
        self.R = {}

    def _deps(self, reads, writes):
        deps = {}
        for k in reads:
            w = self.W.get(k)
            if w is not None:
                deps[w[0]] = max(deps.get(w[0], 0), w[1])
        for k in writes:
            w = self.W.get(k)
            if w is not None:
                deps[w[0]] = max(deps.get(w[0], 0), w[1])
            for r, v in self.R.get(k, {}).items():
                deps[r] = max(deps.get(r, 0), v)
        return deps

    def _wait(self, e, deps, skip_self=False):
        for r, v in deps.items():
            if skip_self and r == e:
                continue
            if self.clock[e].get(r, 0) >= v:
                continue
            self.eng[e].wait_ge(self.sem[r], v)
            self.clock[e][r] = v

    def _mark(self, res, val, reads, writes):
        for k in writes:
            self.W[k] = (res, val)
            self.R[k] = {}
        for k in reads:
            self.R.setdefault(k, {})[res] = val

    def op(self, e, fn, reads=(), writes=()):
        deps = self._deps(reads, writes)
        self._wait(e, deps, skip_self=(e == "pe"))
        ins = fn(self.eng[e])
        self.cnt[e] += 1
        ins.then_inc(self.sem[e], 1)
        self._mark(e, self.cnt[e], reads, writes)
        return ins

    def dma(self, q, out, in_, reads=(), writes=()):
        deps = self._deps(reads, writes)
        self._wait(q, deps)
        lane = self.lanes[q][self.lane_next[q]]
        self.lane_next[q] = (self.lane_next[q] + 1) % self.NL
        if self.cnt[lane] > 0:
            self._wait(q, {lane: self.cnt[lane]})
        ins = self.eng[q].dma_start(out=out, in_=in_)
        self.cnt[lane] += 16
        ins.then_inc(self.sem[lane], 16)
        self._mark(lane, self.cnt[lane], reads, writes)
        return ins

    def allgather(self, in_ap, out_ap, reads=(), writes=()):
        deps = self._deps(reads, writes)
        self._wait("pool", deps)
        if self.cnt["cc"] > 0:
            self._wait("pool", {"cc": self.cnt["cc"]})
        ins = self.nc.gpsimd.collective_compute(
            "AllGather", ALU.bypass, replica_groups=[list(range(NCORES))],
            ins=[in_ap], outs=[out_ap])
        self.cnt["cc"] += 1
        ins.then_inc(self.sem["cc"], 1)
        self._mark("cc", self.cnt["cc"], reads, writes)
        self._wait("pool", {"cc": self.cnt["cc"]})

    def barrier(self):
        allr = {r: v for r, v in self.cnt.items() if v > 0}
        for e in self.CE:
            self._wait(e, dict(allr))

    def finish(self, e="sp"):
        self._wait(e, {r: v for r, v in self.cnt.items() if v > 0})


_UID = [0]


def _uid():
    _UID[0] += 1
    return _UID[0]


class _Stop(Exception):
    pass


class Builder:
    def __init__(self, stages):
        self.stages = stages
        self.stop = 0
        self.pending = []
        self.stopped = False
        for a in stages:
            if a.startswith("stop"):
                self.stop = int(a[4:])

    def _init2(self):
        self.nc = bass.Bass("TRN2", target_bir_lowering=False)
        self.din = {}

    def _suppress(self, et, ev, tb):
        if et is _Stop:
            self.stopped = True
            return True
        return False

    def chk(self, k):
        if self.stop == k:
            raise _Stop()

    def _unused(self):
        pass

    def inp(self, name, shape, dt=F32):
        t = self.nc.dram_tensor(name, list(shape), dt, kind="ExternalInput")
        self.din[name] = t
        return t

    def build(self):
        self._init2()
        nc = self.nc
        x_d = self.inp("x", [TPC, D])
        ffn_w_in = self.inp("ffn_w_in", [DEPTH, D, 2 * DFF])
        ffn_w_out = self.inp("ffn_w_out", [DEPTH, DFF, D])
        ffn_cw = self.inp("ffn_cw", [DEPTH, 128, 2 * NFC, 3])
        ln_g = self.inp("ln_g", [DEPTH * 2, D])
        ln_b = self.inp("ln_b", [DEPTH * 2, D])
        ident_d = self.inp("ident", [128, 128])
        hmask_d = self.inp("hmask", [128, NCORES, 32])
        self.inp("triu", [128, 128])
        sg_w_in = self.inp("sg_w_in", [1, D, 2 * D])
        sg_w_out = self.inp("sg_w_out", [1, D, D])
        sg_wspT = self.inp("sg_wspT", [1, 128, 8, 128])
        sg_bspT = self.inp("sg_bspT", [1, 128, 8])
        sg_ln_g = self.inp("sg_ln_g", [1, D])
        sg_ln_b = self.inp("sg_ln_b", [1, D])
        self.inp("cmask", [128, NCORES])
        self.inp("gla_w_in", [2, D, 3088])
        self.inp("gla_w_gk2", [2, 16, 512])
        self.inp("gla_b_gkT", [2, 128, 4])
        self.inp("gla_norm_wt", [2, D])
        self.inp("gla_w_out", [2, D, D])
        self.inp("trilS", [128, 128])
        self.inp("gdn_w_in", [1, D, 4112])
        self.inp("gdn_cw", [1, 128, 24, 4])
        self.inp("gdn_a_log", [1, 8])
        self.inp("gdn_dt_bias", [1, 8])
        self.inp("gdn_norm_wt", [1, D])
        self.inp("gdn_w_out", [1, D, D])
        gd_in = nc.dram_tensor("gd_in", [128, 2048], F32, kind="Internal")
        gd_all = nc.dram_tensor("gd_all", [NCORES * 128, 2048], F32, kind="Internal", addr_space="Shared")
        wffi = nc.dram_tensor("wffi", [DEPTH, NFC, 128, KC * 256], BF16, kind="Internal")
        wffo = nc.dram_tensor("wffo", [DEPTH, 128, NFC * D], BF16, kind="Internal")
        wgdn = nc.dram_tensor("wgdn", [24, 128, KC * 128], BF16, kind="Internal")
        rtscr = nc.dram_tensor("rtscr", [128, 8, TPC], BF16, kind="Internal")
        oscr = nc.dram_tensor("oscr", [TPC, D], F32, kind="Internal")
        gst_in = nc.dram_tensor("gst_in", [128, 1028], F32, kind="Internal")
        gst_all = nc.dram_tensor("gst_all", [NCORES * 128, 1028], F32, kind="Internal", addr_space="Shared")
        out_d = nc.dram_tensor("out", [TPC, D], F32, kind="ExternalOutput")
        halo_in = nc.dram_tensor("halo_in", [128, 32], BF16, kind="Internal")
        halo_all = nc.dram_tensor("halo_all", [NCORES * 128, 32], BF16, kind="Internal", addr_space="Shared")
        self.d = dict(x=x_d, ffn_w_in=ffn_w_in, ffn_w_out=ffn_w_out, ffn_cw=ffn_cw, ln_g=ln_g, ln_b=ln_b,
                      out=out_d, halo_in=halo_in, halo_all=halo_all)
        self.d.update({k: v for k, v in self.din.items()})
        self.d.update(oscr=oscr, gst_in=gst_in, gst_all=gst_all, gd_in=gd_in, gd_all=gd_all, rtscr=rtscr, wffi=wffi, wffo=wffo, wgdn=wgdn)

        with contextlib.ExitStack() as es:
            self.es = es
            sb = lambda name, shape, dt: es.enter_context(nc.sbuf_tensor("sb_%s_%d" % (name, _uid()), list(shape), dt))
            ps = lambda name, shape, dt: es.enter_context(nc.psum_tensor("ps_" + name, list(shape), dt))
            self.H = sb("H", [128, NT, D], F32)
            self.XT = sb("XT", [128, KC, TPC], BF16)
            self.XTH = sb("XTH", [128, KC, 4], BF16)
            self.HA = sb("HA", [128, NCORES, 32], BF16)
            self.HAm = sb("HAm", [128, NCORES, 32], F32)
            self.XTHf = sb("XTHf", [128, 32], F32)
            self.hmask = sb("hmask", [128, NCORES, 32], BF16)
            self.ident = sb("ident", [128, 128], BF16)
            self.identf = sb("identf", [128, 128], F32)
            self.lng = sb("lng", [128, D], F32)
            self.lnb = sb("lnb", [128, D], F32)
            self.xb = sb("xb", [128, D], BF16)
            self.srcT = sb("srcT", [128, KC, 128], BF16)
            self.triu = sb("triu", [128, 128], F32)
            self.cmask = sb("cmask", [128, NCORES], F32)
            self.ones = sb("ones", [128, 256], F32)
            self.onesf = sb("onesf", [128, 128], F32)
            self.trilS = sb("trilS", [128, 128], F32)
            self.zt = sb("zt", [128, D], F32)
            self.st6 = sb("st6", [128, 12], F32)
            self.mv = sb("mv", [128, 2], F32)
            self.rstd = sb("rstd", [128, 1], F32)
            self.psA = [ps("psA%d" % i, [128, 512], F32) for i in range(4)]
            self.psO = ps("psO", [128, 1024], F32)
            self.psT = ps("psT", [128, KC, 128], BF16)
            self.psM = ps("psM", [128, 512], F32)
            es.enter_context(nc.Block())
            self.fw = FW(nc, es)
            self.body()
            self.fw.finish("sp")
        return nc

    def body(self):
        fw = self.fw
        d = self.d
        fw.dma("pool", self.ident[:], self.din["ident"].ap(), writes=["ident"])
        fw.dma("sp", self.identf[:], self.din["ident"].ap(), writes=["identf"])
        fw.dma("pool", self.hmask[:], self.din["hmask"].ap(), writes=["hmask"])
        fw.dma("sp", self.triu[:], self.din["triu"].ap(), writes=["triu"])
        fw.dma("sp", self.cmask[:], self.din["cmask"].ap(), writes=["cmask"])
        fw.op("dve", lambda e: e.memset(self.ones[:], 1.0), writes=["ones"])
        fw.op("dve", lambda e: e.memset(self.onesf[:], 1.0), writes=["onesf"])
        fw.dma("sp", self.trilS[:], self.din["trilS"].ap(), writes=["trilS"])
        xv = d["x"].ap().rearrange("(t p) f -> p t f", p=128)
        for t in range(NT):
            fw.dma("sp", self.H[:, t, :], xv[:, t, :], writes=[("H", t)])
        st = self.stages
        for t in range(NT):
            self.make_xt(t)
        if "ffn_only" in st:
            self.halo_exchange()
            self.pending += self.conv_ffn_jobs(0)
            self.run_jobs()
            self.ffn(0)
        else:
            layers = list(range(DEPTH)) if "full" in st else [int(a[1:]) for a in st if a.startswith("L")]
            self.layers = layers
            first = True
            for li in layers:
                mixer, j = li % 3, li // 3
                if first:
                    self.pending += self.conv_ffn_jobs(li)
                    if mixer != 0:
                        self.run_jobs()
                if mixer == 0:
                    self.gla(li, j, hook=(self.run_jobs if first else None))
                elif mixer == 1:
                    self.halo_exchange()
                    self.gdn(li, j)
                    if self.stopped:
                        self.fw.barrier()
                else:
                    self.sgu(li, j)
                self.halo_exchange()
                first = False
                nxt = layers[layers.index(li) + 1] if layers.index(li) + 1 < len(layers) else None
                if nxt is not None:
                    self.pending += self.conv_ffn_jobs(nxt)
                self.ffn(li)
                self.run_jobs()
        ov = d["out"].ap().rearrange("(t p) f -> p t f", p=128)
        for t in range(NT):
            fw.dma("sp", ov[:, t, :], self.H[:, t, :], reads=[("H", t)], writes=[("out", t)])

    def make_xt(self, t):
        fw = self.fw
        fw.op("act", lambda e: e.copy(self.xb[:], self.H[:, t, :]), reads=[("H", t)], writes=["xb"])
        for kc in range(KC):
            fw.op("pe", lambda e, kc=kc: e.transpose(self.psT[:, kc, :], self.xb[:, kc * 128:(kc + 1) * 128], self.ident[:]),
                  reads=["xb", "ident"], writes=["psT"])
        fw.op("act", lambda e: e.copy(self.XT[:, :, t * 128:(t + 1) * 128], self.psT[:]),
              reads=["psT"], writes=[("XT", t)])

    def halo_exchange(self):
        fw = self.fw
        d = self.d
        fw.dma("sp", d["halo_in"].ap().rearrange("p (k f) -> p k f", k=KC), self.XT[:, :, TPC - 4:TPC],
               reads=[("XT", NT - 1)], writes=["halo_in"])
        fw.allgather(d["halo_in"].ap(), d["halo_all"].ap(), reads=["halo_in"], writes=["halo_all"])
        fw.dma("pool", self.HA[:], d["halo_all"].ap().rearrange("(c p) f -> p c f", p=128),
               reads=["halo_all"], writes=["HA"])
        fw.op("dve", lambda e: e.tensor_tensor(self.HAm[:], self.HA[:], self.hmask[:], ALU.mult),
              reads=["HA", "hmask"], writes=["HAm"])
        fw.op("dve", lambda e: e.tensor_reduce(self.XTHf[:], self.HAm[:].rearrange("p c f -> p f c"), AX.X, ALU.add),
              reads=["HAm"], writes=["XTHf"])
        fw.op("dve", lambda e: e.tensor_copy(self.XTH[:].rearrange("p k f -> p (k f)"), self.XTHf[:]),
              reads=["XTHf"], writes=["XTH"])

    def layer_norm_tile(self, t, ps_y, li, ykeys):
        fw = self.fw
        fw.op("dve", lambda e: e.scalar_tensor_tensor(self.zt[:], self.H[:, t, :], ALPHA, ps_y, ALU.mult, ALU.add),
              reads=[("H", t)] + ykeys, writes=["zt"])
        for c in range(2):
            fw.op("dve", lambda e, c=c: e.bn_stats(self.st6[:, c * 6:(c + 1) * 6], self.zt[:, c * 512:(c + 1) * 512]),
                  reads=["zt"], writes=[("st6", c)])
        fw.op("dve", lambda e: e.bn_aggr(self.mv[:], self.st6[:]), reads=[("st6", 0), ("st6", 1)], writes=["mv"])
        fw.op("act", lambda e: e.activation(self.rstd[:], self.mv[:, 1:2], AF.Sqrt, bias=LN_EPS, scale=1.0),
              reads=["mv"], writes=["rstd"])
        fw.op("dve", lambda e: e.reciprocal(self.rstd[:], self.rstd[:]), reads=["rstd"], writes=["rstd"])
        fw.op("dve", lambda e: e.tensor_scalar(self.zt[:], self.zt[:], self.mv[:, 0:1], self.rstd[:, 0:1],
                                               ALU.subtract, ALU.mult),
              reads=["zt", "mv", "rstd"], writes=["zt"])
        fw.op("pool", lambda e: e.tensor_tensor(self.zt[:], self.zt[:], self.lng[:], ALU.mult),
              reads=["zt", "lng"], writes=["zt"])
        fw.op("pool", lambda e: e.tensor_tensor(self.H[:, t, :], self.zt[:], self.lnb[:], ALU.add),
              reads=["zt", "lnb"], writes=[("H", t)])


    def out_proj_ln(self, t, src, srckeys, WOm, wkey, li):
        fw = self.fw
        for kc in range(KC):
            fw.op("pe", lambda e, kc=kc: e.transpose(self.psT[:, kc, :], src[:, kc * 128:(kc + 1) * 128], self.ident[:]),
                  reads=srckeys + ["ident"], writes=["psT"])
        fw.op("act", lambda e: e.copy(self.srcT[:], self.psT[:]), reads=["psT"], writes=["srcT"])
        for hf in range(2):
            for kc in range(KC):
                fw.op("pe", lambda e, kc=kc, hf=hf: e.matmul(
                    self.psO[:, hf * 512:(hf + 1) * 512], self.srcT[:, kc, :], WOm[:, kc, hf * 512:(hf + 1) * 512],
                    start=(kc == 0), stop=(kc == KC - 1)), reads=["srcT", wkey], writes=["psO"])
        self.layer_norm_tile(t, self.psO[:], li, ["psO"])
        self.make_xt(t)

    def load_wout(self, WOm, w_ap, wkey):
        v = w_ap.rearrange("(k p) n -> p k n", p=128)
        for kc in range(KC):
            self.fw.dma("pool", WOm[:, kc, :], v[:, kc, :], writes=[wkey] if kc == 0 else [(wkey, kc)])

    def sgu(self, li, j):
        nc = self.nc
        fw = self.fw
        d = self.d
        self.load_ln(li * 2)
        with contextlib.ExitStack() as es:
            sb = lambda name, shape, dt: es.enter_context(nc.sbuf_tensor("sb_%s_%d" % (name, _uid()), list(shape), dt))
            Wi = sb("sgWi", [128, KC, 2 * D], BF16)
            WOm = sb("sgWO", [128, KC, D], BF16)
            Wsp = sb("sgWsp", [128, 8, 128], BF16)
            Wspf = sb("sgWspf", [128, 8, 128], F32)
            bsp = sb("sgbsp", [128, 8], F32)
            sg_g = sb("sg_g", [128, D], F32)
            sg_b = sb("sg_b", [128, D], F32)
            u = sb("sgu", [128, D], F32)
            vz = sb("sgvz", [128, D], F32)
            vn = sb("sgvn", [128, D], BF16)
            um = sb("sgum", [128, D], BF16)
            wv = d["sg_w_in"].ap()[j].rearrange("(k p) n -> p k n", p=128)
            for kc in range(KC):
                fw.dma("pool", Wi[:, kc, :], wv[:, kc, :], writes=[("sgWi", kc)])
            self.load_wout(WOm, d["sg_w_out"].ap()[j], "sgWO")
            fw.dma("sp", Wspf[:], d["sg_wspT"].ap()[j], writes=["Wspf"])
            fw.dma("sp", bsp[:], d["sg_bspT"].ap()[j], writes=["bsp"])
            fw.dma("sp", sg_g[:], d["sg_ln_g"].ap()[j:j + 1, :].partition_broadcast(128), writes=["sg_g"])
            fw.dma("sp", sg_b[:], d["sg_ln_b"].ap()[j:j + 1, :].partition_broadcast(128), writes=["sg_b"])
            fw.op("dve", lambda e: e.tensor_tensor(Wsp[:], Wspf[:], self.triu[:].unsqueeze(1).broadcast_to([128, 8, 128]), ALU.mult),
                  reads=["Wspf", "triu"], writes=["Wsp"])
            wkeys = [("sgWi", kc) for kc in range(KC)]
            for t in range(NT):
                tsl = slice(t * 128, (t + 1) * 128)
                for cb in range(4):
                    for kc in range(KC):
                        fw.op("pe", lambda e, kc=kc, cb=cb: e.matmul(
                            self.psA[cb][:], self.XT[:, kc, tsl], Wi[:, kc, cb * 512:(cb + 1) * 512],
                            start=(kc == 0), stop=(kc == KC - 1)), reads=[("XT", t)] + wkeys, writes=[("psA", cb)])
                for cb in range(2):
                    fw.op("act", lambda e, cb=cb: e.activation(u[:, cb * 512:(cb + 1) * 512], self.psA[cb][:], AF.Gelu),
                          reads=[("psA", cb)], writes=[("sgu", cb)])
                    fw.op("act", lambda e, cb=cb: e.activation(vz[:, cb * 512:(cb + 1) * 512], self.psA[2 + cb][:], AF.Gelu),
                          reads=[("psA", 2 + cb)], writes=["sgvz"])
                for c in range(2):
                    fw.op("dve", lambda e, c=c: e.bn_stats(self.st6[:, c * 6:(c + 1) * 6], vz[:, c * 512:(c + 1) * 512]),
                          reads=["sgvz"], writes=[("st6", c)])
                fw.op("dve", lambda e: e.bn_aggr(self.mv[:], self.st6[:]), reads=[("st6", 0), ("st6", 1)], writes=["mv"])
                fw.op("act", lambda e: e.activation(self.rstd[:], self.mv[:, 1:2], AF.Sqrt, bias=LN_EPS, scale=1.0),
                      reads=["mv"], writes=["rstd"])
                fw.op("dve", lambda e: e.reciprocal(self.rstd[:], self.rstd[:]), reads=["rstd"], writes=["rstd"])
                fw.op("dve", lambda e: e.tensor_scalar(vz[:], vz[:], self.mv[:, 0:1], self.rstd[:, 0:1], ALU.subtract, ALU.mult),
                      reads=["sgvz", "mv", "rstd"], writes=["sgvz"])
                fw.op("pool", lambda e: e.tensor_tensor(vz[:], vz[:], sg_g[:], ALU.mult), reads=["sgvz", "sg_g"], writes=["sgvz"])
                fw.op("pool", lambda e: e.tensor_tensor(vn[:], vz[:], sg_b[:], ALU.add), reads=["sgvz", "sg_b"], writes=["sgvn"])
                for g in range(8):
                    fw.op("pe", lambda e, g=g: e.matmul(self.psA[g // 4][:, (g % 4) * 128:(g % 4 + 1) * 128],
                                                        Wsp[:, g, :], vn[:, g * 128:(g + 1) * 128], start=True, stop=True),
                          reads=["Wsp", "sgvn"], writes=[("psA", g // 4)])
                for g in range(8):
                    fw.op("dve", lambda e, g=g: e.scalar_tensor_tensor(
                        um[:, g * 128:(g + 1) * 128], self.psA[g // 4][:, (g % 4) * 128:(g % 4 + 1) * 128],
                        bsp[:, g:g + 1], u[:, g * 128:(g + 1) * 128], ALU.add, ALU.mult),
                        reads=[("psA", g // 4), "bsp", ("sgu", g // 4)], writes=["sgum"])
                self.out_proj_ln(t, um, ["sgum"], WOm, "sgWO", li * 2)
            fw.barrier()


    def gla(self, li, j, hook=None):
        nc = self.nc
        fw = self.fw
        d = self.d
        NH, DK, DV = 4, 128, 256
        SC = DK ** -0.5
        TB = 256
        NCH = TB // 64
        self.load_ln(li * 2)
        with contextlib.ExitStack() as es_all:
            sba = lambda name, shape, dt: es_all.enter_context(nc.sbuf_tensor("sb_%s_%d" % (name, _uid()), list(shape), dt))
            QET = sba("QET", [128, NH, TPC], BF16)
            S0b = sba("S0b", [128, NH * DV], BF16)
            with contextlib.ExitStack() as es:
                sb = lambda name, shape, dt: es.enter_context(nc.sbuf_tensor("sb_%s_%d" % (name, _uid()), list(shape), dt))
                Wqk = sb("glWqk", [128, KC, 1024], BF16)
                Wv = sb("glWv", [128, KC, 1024], BF16)
                Wlr = sb("glWlr", [128, KC, 16], BF16)
                Wgk2 = sb("glWgk2", [16, 512], BF16)
                nbg = sb("glnbg", [128, NH], F32)
                lrT = sb("gllrT", [16, TB], BF16)
                e1 = sb("gle1", [128, TB], F32)
                sp = sb("glsp", [128, TB], F32)
                cum = sb("glcum", [128, NH, TB + 1], F32)
                nb = sb("glnb", [128, TB], F32)
                dref = sb("gldref", [128, TB], F32)
                dlast = sb("gldlast", [128, TB], F32)
                E = [sb("glE%d" % i, [128, TB], F32) for i in range(5)]
                dec = sb("gldec", [128, NH, NCH], F32)
                qp = sb("glqp", [128, NH, TB], BF16)
                qpp = sb("glqpp", [128, NH, TB], BF16)
                kp = sb("glkp", [128, NH, TB], BF16)
                kpp = sb("glkpp", [128, NH, TB], BF16)
                vb = sb("glvb", [64, NH * DV], BF16)
                AT = sb("glAT", [64, NH, 64], BF16)
                kT = sb("glkT", [64, NH, 128], BF16)
                oloc = sb("gloloc", [64, NH * DV], F32)
                S = sb("glS", [128, NH * DV], F32)
                Sb = sb("glSb", [128, NH * DV], BF16)
                dtot = sb("gldtot", [128, NH], F32)
                wv_ = d["gla_w_in"].ap()[j].rearrange("(k p) n -> p k n", p=128)
                for kc in range(KC):
                    fw.dma("pool", Wqk[:, kc, :], wv_[:, kc, 0:1024], writes=[("glWqk", kc)])
                    fw.dma("pool", Wv[:, kc, :], wv_[:, kc, 1024:2048], writes=[("glWv", kc)])
                fw.dma("pool", Wlr[:], wv_[:, :, 3072:3088], writes=["glWlr"])
                fw.dma("pool", Wgk2[:], d["gla_w_gk2"].ap()[j], writes=["glWgk2"])
                fw.dma("sp", nbg[:], d["gla_b_gkT"].ap()[j], writes=["glnbg"])
                if hook is not None:
                    hook()
                fw.op("dve", lambda e: e.tensor_scalar(nbg[:], nbg[:], -1.0, None, ALU.mult), reads=["glnbg"], writes=["glnbg"])
                fw.op("dve", lambda e: e.memset(S[:], 0.0), writes=["glS"])
                fw.op("dve", lambda e: e.memset(Sb[:], 0.0), writes=["glSb"])
                fw.op("dve", lambda e: e.memset(cum[:], 0.0), writes=[("glcum", h) for h in range(NH)])
                wqk_keys = [("glWqk", kc) for kc in range(KC)]
                wv_keys = [("glWv", kc) for kc in range(KC)]
                for b in range(TPC // TB):
                    bsl = slice(b * TB, (b + 1) * TB)
                    xkeys = [("XT", t) for t in range(b * 2, b * 2 + 2)]
                    for kc in range(KC):
                        fw.op("pe", lambda e, kc=kc: e.matmul(self.psM[0:16, 0:TB], Wlr[:, kc, :], self.XT[:, kc, bsl],
                                                             start=(kc == 0), stop=(kc == KC - 1)),
                              reads=["glWlr"] + xkeys, writes=["psM"])
                    fw.op("act", lambda e: e.copy(lrT[:], self.psM[0:16, 0:TB]), reads=["psM"], writes=["gllrT"])
                    for h in range(NH):
                        hs = slice(h * 128, (h + 1) * 128)
                        fw.op("pe", lambda e, hs=hs: e.matmul(self.psA[0][:, 0:TB], Wgk2[:, hs], lrT[:], start=True, stop=True),
                              reads=["glWgk2", "gllrT"], writes=[("psA", 0)])
                        fw.op("act", lambda e, h=h: e.activation(e1[:], self.psA[0][:, 0:TB], AF.Exp, bias=nbg[:, h:h + 1], scale=-1.0),
                              reads=[("psA", 0), "glnbg"], writes=["gle1"])
                        fw.op("act", lambda e: e.activation(sp[:], e1[:], AF.Ln, bias=1.0, scale=1.0), reads=["gle1"], writes=["glsp"])
                        fw.op("dve", lambda e, h=h: e.tensor_tensor_scan(cum[:, h, 1:TB + 1], self.ones[:, 0:TB], sp[:],
                                                                        cum[:, h, 0:1], ALU.mult, ALU.add),
                              reads=["glsp", "ones", ("glcum", h)], writes=[("glcum", h)])
                        cprev = cum[:, h, 0:TB].rearrange("p (c s) -> p c s", s=64)[:, :, 0:1].broadcast_to([128, NCH, 64])
                        nb3 = nb[:].rearrange("p (c s) -> p c s", s=64)
                        fw.op("dve", lambda e, h=h, cprev=cprev: e.tensor_tensor(
                            nb3, cum[:, h, 1:TB + 1].rearrange("p (c s) -> p c s", s=64), cprev, ALU.subtract),
                            reads=[("glcum", h)], writes=["glnb"])
                        fw.op("dve", lambda e: e.tensor_tensor(dref[:].rearrange("p (c s) -> p c s", s=64), nb3,
                                                               nb3[:, :, 32:33].broadcast_to([128, NCH, 64]), ALU.subtract),
                              reads=["glnb"], writes=["gldref"])
                        fw.op("dve", lambda e: e.tensor_tensor(dlast[:].rearrange("p (c s) -> p c s", s=64), nb3,
                                                               nb3[:, :, 63:64].broadcast_to([128, NCH, 64]), ALU.subtract),
                              reads=["glnb"], writes=["gldlast"])
                        fw.op("act", lambda e: e.activation(E[0][:], dref[:], AF.Exp, scale=-1.0 / 16), reads=["gldref"], writes=[("glE", 0)])
                        fw.op("act", lambda e: e.activation(E[1][:], dref[:], AF.Exp, scale=1.0 / 16), reads=["gldref"], writes=[("glE", 1)])
                        fw.op("act", lambda e: e.activation(E[2][:], dlast[:], AF.Exp, scale=1.0 / 16), reads=["gldlast"], writes=[("glE", 2)])
                        fw.op("act", lambda e: e.activation(E[3][:], nb[:], AF.Exp, scale=-1.0 / 16), reads=["glnb"], writes=[("glE", 3)])
                        fw.op("act", lambda e, h=h: e.activation(E[4][:], cum[:, h, 1:TB + 1], AF.Exp, scale=-1.0 / 16),
                              reads=[("glcum", h)], writes=[("glE", 4)])
                        fw.op("act", lambda e, h=h: e.activation(dec[:, h, :], nb3[:, :, 63], AF.Exp, scale=-1.0 / 16),
                              reads=["glnb"], writes=[("gldec", h)])
                        fw.op("dve", lambda e, h=h: e.tensor_copy(cum[:, h, 0:1], cum[:, h, TB:TB + 1]),
                              reads=[("glcum", h)], writes=[("glcum", h)])
                        for which, col0 in ((1, h * 128), (2, 512 + h * 128)):
                            for kc in range(KC):
                                fw.op("pe", lambda e, kc=kc, which=which, col0=col0: e.matmul(
                                    self.psA[which][:, 0:TB], Wqk[:, kc, col0:col0 + 128], self.XT[:, kc, bsl],
                                    start=(kc == 0), stop=(kc == KC - 1)), reads=wqk_keys + xkeys, writes=[("psA", which)])
                        stt = lambda out, ps_, sc, Ei: (lambda e: e.scalar_tensor_tensor(out, ps_, sc, Ei, ALU.mult, ALU.mult))
                        fw.op("dve", stt(qp[:, h, :], self.psA[1][:, 0:TB], SC, E[0][:]), reads=[("psA", 1), ("glE", 0)], writes=[("glqp", h)])
                        fw.op("dve", stt(qpp[:, h, :], self.psA[1][:, 0:TB], SC, E[3][:]), reads=[("psA", 1), ("glE", 3)], writes=[("glqpp", h)])
                        fw.op("dve", stt(QET[:, h, bsl], self.psA[1][:, 0:TB], SC, E[4][:]), reads=[("psA", 1), ("glE", 4)], writes=[("QET", (b * TB) // 512)])
                        fw.op("dve", stt(kp[:, h, :], self.psA[2][:, 0:TB], 1.0, E[1][:]), reads=[("psA", 2), ("glE", 1)], writes=[("glkp", h)])
                        fw.op("dve", stt(kpp[:, h, :], self.psA[2][:, 0:TB], 1.0, E[2][:]), reads=[("psA", 2), ("glE", 2)], writes=[("glkpp", h)])
                    allh = lambda nm: [(nm, h) for h in range(NH)]
                    for c in range(TB // 64):
                        csl = slice(c * 64, (c + 1) * 64)
                        tok0 = b * TB + c * 64
                        gsl = slice(tok0, tok0 + 64)
                        for hf in range(2):
                            for kc in range(KC):
                                fw.op("pe", lambda e, kc=kc, hf=hf: e.matmul(
                                    self.psO[0:64, hf * 512:(hf + 1) * 512], self.XT[:, kc, gsl], Wv[:, kc, hf * 512:(hf + 1) * 512],
                                    start=(kc == 0), stop=(kc == KC - 1)), reads=wv_keys + xkeys, writes=["psO"])
                        fw.op("act", lambda e: e.copy(vb[:], self.psO[0:64, :]), reads=["psO"], writes=["glvb"])
                        for h in range(NH):
                            fw.op("pe", lambda e, h=h: e.matmul(self.psA[0][0:64, h * 64:(h + 1) * 64], kp[:, h, csl], qp[:, h, csl],
                                                                start=True, stop=True),
                                  reads=[("glkp", h), ("glqp", h)], writes=[("psA", 0)])
                        fw.op("dve", lambda e: e.tensor_tensor(
                            AT[:], self.psA[0][0:64, 0:256].rearrange("p (h i) -> p h i", h=NH),
                            self.triu[0:64, 0:64].unsqueeze(1).broadcast_to([64, NH, 64]), ALU.mult),
                            reads=[("psA", 0), "triu"], writes=["glAT"])
                        for h in range(NH):
                            fw.op("pe", lambda e, h=h: e.transpose(self.psT[0:64, h, :], kpp[:, h, csl], self.ident[:]),
                                  reads=[("glkpp", h), "ident"], writes=["psT"])
                        fw.op("act", lambda e: e.copy(kT[:], self.psT[0:64, 0:NH, :]), reads=["psT"], writes=["glkT"])
                        for h in range(NH):
                            ob = self.psA[2 + h // 2][0:64, (h % 2) * 256:(h % 2 + 1) * 256]
                            fw.op("pe", lambda e, h=h, ob=ob: e.matmul(ob, AT[:, h, :], vb[:, h * DV:(h + 1) * DV], start=True, stop=False),
                                  reads=["glAT", "glvb"], writes=[("psA", 2 + h // 2)])
                            fw.op("pe", lambda e, h=h, ob=ob: e.matmul(ob, qpp[:, h, csl], Sb[:, h * DV:(h + 1) * DV], start=False, stop=True),
                                  reads=[("glqpp", h), "glSb"], writes=[("psA", 2 + h // 2)])
                        for hf in range(2):
                            fw.op("act", lambda e, hf=hf: e.copy(oloc[:, hf * 512:(hf + 1) * 512], self.psA[2 + hf][0:64, :]),
                                  reads=[("psA", 2 + hf)], writes=["gloloc"])
                        fw.dma("sp", d["oscr"].ap()[tok0:tok0 + 64, :], oloc[:], reads=["gloloc"], writes=[("oscr", tok0 // 128)])
                        for h in range(NH):
                            fw.op("pe", lambda e, h=h: e.matmul(self.psO[:, h * DV:(h + 1) * DV], kT[:, h, :], vb[:, h * DV:(h + 1) * DV],
                                                                start=True, stop=True), reads=["glkT", "glvb"], writes=["psO"])
                        for h in range(NH):
                            fw.op("dve", lambda e, h=h, c=c: e.scalar_tensor_tensor(
                                S[:, h * DV:(h + 1) * DV], S[:, h * DV:(h + 1) * DV], dec[:, h, c:c + 1], self.psO[:, h * DV:(h + 1) * DV],
                                ALU.mult, ALU.add), reads=["glS", ("gldec", h), "psO"], writes=["glS"])
                        fw.op("act", lambda e: e.copy(Sb[:], S[:]), reads=["glS"], writes=["glSb"])
                fw.op("act", lambda e: e.activation(dtot[:], cum[:, :, 0], AF.Exp, scale=-1.0 / 16),
                      reads=[("glcum", h) for h in range(NH)], writes=["gldtot"])
                fw.dma("sp", d["gst_in"].ap()[:, 0:NH * DV], S[:], reads=["glS"], writes=["gst_in"])
                fw.dma("sp", d["gst_in"].ap()[:, NH * DV:NH * DV + NH], dtot[:], reads=["gldtot"], writes=["gst_in2"])
                fw.allgather(d["gst_in"].ap(), d["gst_all"].ap(), reads=["gst_in", "gst_in2"], writes=["gst_all"])
                fw.barrier()
            with contextlib.ExitStack() as es:
                sb = lambda name, shape, dt: es.enter_context(nc.sbuf_tensor("sb_%s_%d" % (name, _uid()), list(shape), dt))
                Sall = sb("glSall", [128, NCORES, NH * DV + NH], F32)
                st = sb("glstt", [128, NH * DV], F32)
                S0 = sb("glS0", [128, NH * DV], F32)
                fw.dma("sp", Sall[:], d["gst_all"].ap().rearrange("(c p) f -> p c f", p=128), reads=["gst_all"], writes=["glSall"])
                fw.op("dve", lambda e: e.memset(st[:], 0.0), writes=["glstt"])
                fw.op("dve", lambda e: e.memset(S0[:], 0.0), writes=["glS0"])
                for c in range(NCORES):
                    fw.op("dve", lambda e, c=c: e.scalar_tensor_tensor(S0[:], st[:], self.cmask[:, c:c + 1], S0[:], ALU.mult, ALU.add),
                          reads=["glstt", "cmask", "glS0"], writes=["glS0"])
                    if c < NCORES - 1:
                        for h in range(NH):
                            fw.op("dve", lambda e, c=c, h=h: e.scalar_tensor_tensor(
                                st[:, h * DV:(h + 1) * DV], st[:, h * DV:(h + 1) * DV], Sall[:, c, NH * DV + h:NH * DV + h + 1],
                                Sall[:, c, h * DV:(h + 1) * DV], ALU.mult, ALU.add), reads=["glstt", "glSall"], writes=["glstt"])
                fw.op("act", lambda e: e.copy(S0b[:], S0[:]), reads=["glS0"], writes=["S0b"])
                fw.barrier()
            self.mix_tail(li, NH, DV, QET, S0b, d["gla_w_in"].ap()[j][:, 2048:3072], d["gla_w_out"].ap()[j],
                          d["gla_norm_wt"].ap()[j:j + 1, :])


    def gdn(self, li, j):
        nc = self.nc
        fw = self.fw
        d = self.d
        NH, DK, DV, DA = 8, 128, 128, 256
        TB = 256
        NCH = TB // 64
        self.load_ln(li * 2)
        with contextlib.ExitStack() as es_all:
            sba = lambda name, shape, dt: es_all.enter_context(nc.sbuf_tensor("sb_%s_%d" % (name, _uid()), list(shape), dt))
            RTs = sba("gdRTs", [128, NH, 64], BF16)
            S0b = sba("S0b", [128, NH * DV], BF16)
            es_all.push(self._suppress)
            with contextlib.ExitStack() as es:
                sb = lambda name, shape, dt: es.enter_context(nc.sbuf_tensor("sb_%s_%d" % (name, _uid()), list(shape), dt))
                Wc = [sb("gdWc%d" % i, [128, KC, 128], BF16) for i in range(2)]
                Wba = sb("gdWba", [128, KC, 16], BF16)
                cw = sb("gdcw", [128, 24, 4], F32)
                carry = sb("gdcarry", [128, 24, 3], F32)
                hb = sb("gdhb", [128, 3 + TB], F32)
                acc = sb("gdacc", [128, TB], F32)
                qkf = sb("gdqkf", [128, TB], F32)
                sq = sb("gdsq", [128, TB], F32)
                rinv = sb("gdrinv", [128, TB], F32)
                qT = sb("gdqT", [128, NH, TB], BF16)
                kT = sb("gdkT", [128, NH, TB], BF16)
                vT = sb("gdvT", [128, NH, TB], BF16)
                dtb = sb("gddtb", [64, NH], F32)
                negA = sb("gdnegA", [64, NH], F32)
                ba = sb("gdba", [64, 16], F32)
                beta = sb("gdbeta", [64, NH], F32)
                g = sb("gdg", [64, NH], F32)
                bcum = sb("gdbcum", [64, NH], F32)
                eb = sb("gdeb", [64, NH], F32)
                ebl = sb("gdebl", [64, NH], F32)
                beb = sb("gdbeb", [64, NH], F32)
                cd = sb("gdcd", [128, NH], F32)
                gU = sb("gdgU", [64, NH, 64], F32)
                decS = sb("gddecS", [64, NH, 64], F32)
                decT = sb("gddecT", [64, NH, 64], F32)
                P = sb("gdP", [64, NH, 64], F32)
                PT = sb("gdPT", [64, NH, 64], F32)
                TT = sb("gdTT", [64, NH, 64], F32)
                QKTm = sb("gdQKTm", [64, NH, 64], BF16)
                bv = sb("gdbv", [64, NH, DV], F32)
                kbe = sb("gdkbe", [64, NH, DK], F32)
                kd = sb("gdkd", [64, NH, DK], BF16)
                u = sb("gdu", [64, NH, DV], F32)
                wT = sb("gdwT", [128, NH, 64], BF16)
                vnew = sb("gdvnew", [64, NH, DA], BF16)
                oa = sb("gdoa", [64, 4, DA], F32)
                Rb = sb("gdRb", [64, NH, DK], BF16)
                S = sb("gdS", [128, NH, DA], F32)
                Sb = sb("gdSb", [128, NH, DA], BF16)
                es.push(self._suppress)
                wi = d["gdn_w_in"].ap()[j]
                fw.dma("pool", Wba[:], wi[:, 4096:4112].rearrange("(k p) c -> p k c", p=128), writes=["gdWba"])
                fw.dma("sp", cw[:], d["gdn_cw"].ap()[j], writes=["gdcw"])
                fw.dma("sp", dtb[:], d["gdn_dt_bias"].ap()[j:j + 1, :].partition_broadcast(64), writes=["gddtb"])
                fw.dma("sp", negA[:], d["gdn_a_log"].ap()[j:j + 1, :].partition_broadcast(64), writes=["gdnegA"])
                fw.op("act", lambda e: e.activation(negA[:], negA[:], AF.Exp), reads=["gdnegA"], writes=["gdnegA"])
                fw.op("dve", lambda e: e.tensor_scalar(negA[:], negA[:], -1.0, None, ALU.mult), reads=["gdnegA"], writes=["gdnegA"])
                fw.op("dve", lambda e: e.memset(S[:], 0.0), writes=["gdS"])
                for h in range(NH):
                    fw.op("dve", lambda e, h=h: e.tensor_copy(S[:, h, DV:DA], self.identf[:]), reads=["identf", "gdS"], writes=["gdS"])
                fw.op("act", lambda e: e.copy(Sb[:], S[:]), reads=["gdS"], writes=[("gdSb", 0), ("gdSb", 1)])
                bc3 = lambda ap_, n: ap_.unsqueeze(2).broadcast_to([64, NH, n])
                step = 0
                for b in range(TPC // TB):
                    bsl = slice(b * TB, (b + 1) * TB)
                    xkeys = [("XT", t) for t in range(b * 2, b * 2 + 2)]
                    for cc in range(24):
                        ws = step % 2
                        step += 1
                        fw.dma("pool", Wc[ws][:], wi[:, cc * 128:(cc + 1) * 128].rearrange("(k p) c -> p k c", p=128),
                               writes=[("gdWc", ws)])
                        pst = self.psA[cc % 2]
                        pk = ("psA", cc % 2)
                        for kc in range(KC):
                            fw.op("pe", lambda e, kc=kc, ws=ws, pst=pst: e.matmul(pst[:, 0:TB], Wc[ws][:, kc, :], self.XT[:, kc, bsl],
                                                                                 start=(kc == 0), stop=(kc == KC - 1)),
                                  reads=[("gdWc", ws)] + xkeys, writes=[pk])
                        if b == 0:
                            for kc in range(KC):
                                fw.op("pe", lambda e, kc=kc, ws=ws: e.matmul(self.psM[:, 0:3], Wc[ws][:, kc, :], self.XTH[:, kc, 1:4],
                                                                            start=(kc == 0), stop=(kc == KC - 1)),
                                      reads=[("gdWc", ws), "XTH"], writes=["psM"])
                            fw.op("act", lambda e: e.copy(hb[:, 0:3], self.psM[:, 0:3]), reads=["psM"], writes=["gdhb"])
                        else:
                            fw.op("act", lambda e, cc=cc: e.copy(hb[:, 0:3], carry[:, cc, :]), reads=[("gdcarry", cc)], writes=["gdhb"])
                        fw.op("act", lambda e, pst=pst: e.copy(hb[:, 3:3 + TB], pst[:, 0:TB]), reads=[pk], writes=["gdhb"])
                        fw.op("act", lambda e, cc=cc: e.copy(carry[:, cc, :], hb[:, TB:TB + 3]), reads=["gdhb"], writes=[("gdcarry", cc)])
                        fw.op("dve", lambda e, cc=cc: e.tensor_scalar(acc[:], hb[:, 3:3 + TB], cw[:, cc, 3:4], None, ALU.mult),
                              reads=["gdhb", "gdcw"], writes=["gdacc"])
                        for tap in (2, 1, 0):
                            fw.op("dve", lambda e, cc=cc, tap=tap: e.scalar_tensor_tensor(
                                acc[:], hb[:, tap:tap + TB], cw[:, cc, tap:tap + 1], acc[:], ALU.mult, ALU.add),
                                reads=["gdhb", "gdcw", "gdacc"], writes=["gdacc"])
                        h = cc % 8
                        if cc >= 16:
                            fw.op("act", lambda e, h=h: e.activation(vT[:, h, :], acc[:], AF.Silu), reads=["gdacc"], writes=[("gdvT", h)])
                        else:
                            fw.op("act", lambda e: e.activation(qkf[:], acc[:], AF.Silu), reads=["gdacc"], writes=["gdqkf"])
                            fw.op("dve", lambda e: e.tensor_tensor(sq[:], qkf[:], qkf[:], ALU.mult), reads=["gdqkf"], writes=["gdsq"])
                            fw.op("pe", lambda e: e.matmul(self.psA[2][:, 0:TB], self.onesf[:], sq[:], start=True, stop=True),
                                  reads=["onesf", "gdsq"], writes=[("psA", 2)])
                            fw.op("act", lambda e: e.activation(rinv[:], self.psA[2][:, 0:TB], AF.Sqrt, bias=RMS_EPS, scale=1.0),
                                  reads=[("psA", 2)], writes=["gdrinv"])
                            fw.op("dve", lambda e: e.reciprocal(rinv[:], rinv[:]), reads=["gdrinv"], writes=["gdrinv"])
                            dst, key, sc = (qT, "gdqT", DK ** -0.5) if cc < 8 else (kT, "gdkT", 1.0)
                            fw.op("dve", lambda e, dst=dst, h=h, sc=sc: e.scalar_tensor_tensor(dst[:, h, :], qkf[:], sc, rinv[:], ALU.mult, ALU.mult),
                                  reads=["gdqkf", "gdrinv"], writes=[(key, h)])
                    allh = lambda nm: [(nm, h) for h in range(NH)]
                    self.chk(1)
                    for c in range(NCH):
                        csl = slice(c * 64, (c + 1) * 64)
                        tok0 = b * TB + c * 64
                        gsl = slice(tok0, tok0 + 64)
                        for kc in range(KC):
                            fw.op("pe", lambda e, kc=kc: e.matmul(self.psM[0:64, 0:16], self.XT[:, kc, gsl], Wba[:, kc, :],
                                                                 start=(kc == 0), stop=(kc == KC - 1)),
                                  reads=["gdWba"] + xkeys, writes=["psM"])
                        fw.op("act", lambda e: e.copy(ba[:], self.psM[0:64, 0:16]), reads=["psM"], writes=["gdba"])
                        fw.op("act", lambda e: e.activation(beta[:], ba[:, 0:8], AF.Exp, scale=-1.0), reads=["gdba"], writes=["gdbeta"])
                        fw.op("dve", lambda e: e.tensor_scalar(beta[:], beta[:], 1.0, None, ALU.add), reads=["gdbeta"], writes=["gdbeta"])
                        fw.op("dve", lambda e: e.reciprocal(beta[:], beta[:]), reads=["gdbeta"], writes=["gdbeta"])
                        fw.op("dve", lambda e: e.tensor_tensor(g[:], ba[:, 8:16], dtb[:], ALU.add), reads=["gdba", "gddtb"], writes=["gdg"])
                        fw.op("act", lambda e: e.activation(g[:], g[:], AF.Exp), reads=["gdg"], writes=["gdg"])
                        fw.op("act", lambda e: e.activation(g[:], g[:], AF.Ln, bias=1.0, scale=1.0), reads=["gdg"], writes=["gdg"])
                        fw.op("dve", lambda e: e.tensor_tensor(g[:], g[:], negA[:], ALU.mult), reads=["gdg", "gdnegA"], writes=["gdg"])
                        fw.op("pe", lambda e: e.matmul(self.psM[0:64, 16:24], self.triu[0:64, 0:64], g[:], start=True, stop=True),
                              reads=["triu", "gdg"], writes=["psM"])
                        fw.op("pe", lambda e: e.matmul(self.psM[:, 24:32], self.onesf[0:64, :], g[:], start=True, stop=True),
                              reads=["onesf", "gdg"], writes=["psM"])
                        fw.op("act", lambda e: e.copy(bcum[:], self.psM[0:64, 16:24]), reads=["psM"], writes=["gdbcum"])
                        fw.op("act", lambda e: e.activation(eb[:], self.psM[0:64, 16:24], AF.Exp), reads=["psM"], writes=["gdeb"])
                        fw.op("act", lambda e: e.activation(cd[:], self.psM[:, 24:32], AF.Exp), reads=["psM"], writes=["gdcd"])
                        fw.op("dve", lambda e: e.tensor_tensor(ebl[:], self.psM[0:64, 24:32], bcum[:], ALU.subtract),
                              reads=["psM", "gdbcum"], writes=["gdebl"])
                        fw.op("act", lambda e: e.activation(ebl[:], ebl[:], AF.Exp), reads=["gdebl"], writes=["gdebl"])
                        fw.op("dve", lambda e: e.tensor_tensor(beb[:], beta[:], eb[:], ALU.mult), reads=["gdbeta", "gdeb"], writes=["gdbeb"])
                        for h in range(NH):
                            fw.op("dve", lambda e, h=h: e.tensor_scalar(gU[:, h, :], self.triu[0:64, 0:64], g[:, h:h + 1], None, ALU.mult),
                                  reads=["triu", "gdg"], writes=["gdgU"])
                        for h in range(NH):
                            fw.op("pe", lambda e, h=h: e.matmul(self.psA[0][0:64, h * 64:(h + 1) * 64], gU[:, h, :], self.trilS[0:64, 0:64],
                                                                start=True, stop=True), reads=["gdgU", "trilS"], writes=[("psA", 0)])
                            fw.op("pe", lambda e, h=h: e.matmul(self.psA[1][0:64, h * 64:(h + 1) * 64], self.trilS[0:64, 0:64], gU[:, h, :],
                                                                start=True, stop=True), reads=["gdgU", "trilS"], writes=[("psA", 1)])
                        v3 = lambda ps_: ps_[0:64, :].rearrange("p (h i) -> p h i", h=NH)
                        fw.op("act", lambda e: e.activation(decS[:], v3(self.psA[0]), AF.Exp), reads=[("psA", 0)], writes=["gddecS"])
                        fw.op("act", lambda e: e.activation(decT[:], v3(self.psA[1]), AF.Exp), reads=[("psA", 1)], writes=["gddecT"])
                        fw.op("dve", lambda e: e.tensor_tensor(decS[:], decS[:], self.trilS[0:64, 0:64].unsqueeze(1).broadcast_to([64, NH, 64]), ALU.mult),
                              reads=["gddecS", "trilS"], writes=["gddecS"])
                        fw.op("dve", lambda e: e.tensor_tensor(decS[:], decS[:], bc3(beta[:], 64), ALU.mult),
                              reads=["gddecS", "gdbeta"], writes=["gddecS"])
                        fw.op("dve", lambda e: e.tensor_tensor(decT[:], decT[:], self.triu[0:64, 0:64].unsqueeze(1).broadcast_to([64, NH, 64]), ALU.mult),
                              reads=["gddecT", "triu"], writes=["gddecT"])
                        for h in range(NH):
                            fw.op("pe", lambda e, h=h: e.matmul(self.psA[2][0:64, h * 64:(h + 1) * 64], kT[:, h, csl], kT[:, h, csl],
                                                                start=True, stop=True), reads=[("gdkT", h)], writes=[("psA", 2)])
                            fw.op("pe", lambda e, h=h: e.matmul(self.psA[3][0:64, h * 64:(h + 1) * 64], kT[:, h, csl], qT[:, h, csl],
                                                                start=True, stop=True), reads=[("gdkT", h), ("gdqT", h)], writes=[("psA", 3)])
                        fw.op("dve", lambda e: e.scalar_tensor_tensor(P[:], v3(self.psA[2]), -1.0, decS[:], ALU.mult, ALU.mult),
                              reads=[("psA", 2), "gddecS"], writes=["gdP"])
                        fw.op("dve", lambda e: e.tensor_tensor(QKTm[:], v3(self.psA[3]), decT[:], ALU.mult),
                              reads=[("psA", 3), "gddecT"], writes=["gdQKTm"])
                        for h in range(NH):
                            fw.op("pe", lambda e, h=h: e.transpose(self.psA[0][0:64, h * 64:(h + 1) * 64], P[:, h, :], self.identf[0:64, 0:64]),
                                  reads=["gdP", "identf"], writes=[("psA", 0)])
                        fw.op("act", lambda e: e.copy(PT[:], v3(self.psA[0])), reads=[("psA", 0)], writes=["gdPT"])
                        self.chk(2)
                        fw.op("dve", lambda e: e.tensor_tensor(TT[:], PT[:], self.identf[0:64, 0:64].unsqueeze(1).broadcast_to([64, NH, 64]), ALU.add),
                              reads=["gdPT", "identf"], writes=["gdTT"])
                        for lvl in range(5):
                            for h in range(NH):
                                fw.op("pe", lambda e, h=h: e.matmul(self.psA[0][0:64, h * 64:(h + 1) * 64], PT[:, h, :], P[:, h, :],
                                                                    start=True, stop=True), reads=["gdP", "gdPT"], writes=[("psA", 0)])
                                fw.op("pe", lambda e, h=h: e.matmul(self.psA[1][0:64, h * 64:(h + 1) * 64], P[:, h, :], PT[:, h, :],
                                                                    start=True, stop=True), reads=["gdP", "gdPT"], writes=[("psA", 1)])
                            fw.op("act", lambda e: e.copy(P[:], v3(self.psA[0])), reads=[("psA", 0)], writes=["gdP"])
                            fw.op("act", lambda e: e.copy(PT[:], v3(self.psA[1])), reads=[("psA", 1)], writes=["gdPT"])
                            for h in range(NH):
                                fw.op("pe", lambda e, h=h: e.matmul(self.psA[2][0:64, h * 64:(h + 1) * 64], P[:, h, :], TT[:, h, :],
                                                                    start=True, stop=True), reads=["gdP", "gdTT"], writes=[("psA", 2)])
                            fw.op("dve", lambda e: e.tensor_tensor(TT[:], TT[:], v3(self.psA[2]), ALU.add), reads=["gdTT", ("psA", 2)], writes=["gdTT"])
                        self.chk(3)
                        for h in range(NH):
                            fw.op("pe", lambda e, h=h: e.transpose(self.psT[0:64, h, :], vT[:, h, csl], self.ident[:]),
                                  reads=[("gdvT", h), "ident"], writes=["psT"])
                        fw.op("dve", lambda e: e.tensor_tensor(bv[:], self.psT[0:64, :, :], bc3(beta[:], DV), ALU.mult),
                              reads=["psT", "gdbeta"], writes=["gdbv"])
                        for h in range(NH):
                            fw.op("pe", lambda e, h=h: e.transpose(self.psT[0:64, h, :], kT[:, h, csl], self.ident[:]),
                                  reads=[("gdkT", h), "ident"], writes=["psT"])
                        fw.op("dve", lambda e: e.tensor_tensor(kbe[:], self.psT[0:64, :, :], bc3(beb[:], DK), ALU.mult),
                              reads=["psT", "gdbeb"], writes=["gdkbe"])
                        fw.op("dve", lambda e: e.tensor_tensor(kd[:], self.psT[0:64, :, :], bc3(ebl[:], DK), ALU.mult),
                              reads=["psT", "gdebl"], writes=["gdkd"])
                        for h in range(NH):
                            fw.op("pe", lambda e, h=h: e.matmul(self.psO[0:64, h * DV:(h + 1) * DV], TT[:, h, :], bv[:, h, :], start=True, stop=True),
                                  reads=["gdTT", "gdbv"], writes=["psO"])
                            fw.op("pe", lambda e, h=h: e.matmul(self.psA[3][:, h * 64:(h + 1) * 64], kbe[:, h, :], TT[:, h, :], start=True, stop=True),
                                  reads=["gdTT", "gdkbe"], writes=[("psA", 3)])
                        fw.op("act", lambda e: e.copy(u[:], self.psO[0:64, :].rearrange("p (h v) -> p h v", h=NH)), reads=["psO"], writes=["gdu"])
                        fw.op("act", lambda e: e.copy(wT[:], self.psA[3][:].rearrange("p (h s) -> p h s", h=NH)), reads=[("psA", 3)], writes=["gdwT"])
                        self.chk(4)
                        for hh in range(2):
                            hs = range(hh * 4, hh * 4 + 4)
                            for i, h in enumerate(hs):
                                fw.op("pe", lambda e, h=h, i=i: e.matmul(self.psO[0:64, i * DA:(i + 1) * DA], wT[:, h, :], Sb[:, h, :], start=True, stop=True),
                                      reads=["gdwT", ("gdSb", hh)], writes=["psO"])
                            wS = self.psO[0:64, :].rearrange("p (h a) -> p h a", h=4)
                            fw.op("dve", lambda e, hh=hh, wS=wS: e.tensor_tensor(vnew[:, hh * 4:hh * 4 + 4, 0:DV], u[:, hh * 4:hh * 4 + 4, :], wS[:, :, 0:DV], ALU.subtract),
                                  reads=["gdu", "psO"], writes=[("gdvnew", hh)])
                            fw.op("dve", lambda e, hh=hh, wS=wS: e.tensor_scalar(vnew[:, hh * 4:hh * 4 + 4, DV:DA], wS[:, :, DV:DA], -1.0, None, ALU.mult),
                                  reads=["psO"], writes=[("gdvnew", hh)])
                            for i, h in enumerate(hs):
                                p1 = self.psA[i // 2][0:64, (i % 2) * DA:(i % 2 + 1) * DA]
                                p2 = self.psA[2 + i // 2][0:64, (i % 2) * DA:(i % 2 + 1) * DA]
                                fw.op("pe", lambda e, h=h, p1=p1: e.matmul(p1, qT[:, h, csl], Sb[:, h, :], start=True, stop=True),
                                      reads=[("gdqT", h), ("gdSb", hh)], writes=[("psA", i // 2)])
                                fw.op("pe", lambda e, h=h, p2=p2: e.matmul(p2, QKTm[:, h, :], vnew[:, h, :], start=True, stop=True),
                                      reads=["gdQKTm", ("gdvnew", hh)], writes=[("psA", 2 + i // 2)])
                            for i, h in enumerate(hs):
                                p1 = self.psA[i // 2][0:64, (i % 2) * DA:(i % 2 + 1) * DA]
                                p2 = self.psA[2 + i // 2][0:64, (i % 2) * DA:(i % 2 + 1) * DA]
                                fw.op("act", lambda e, i=i, p2=p2: e.copy(oa[:, i, :], p2), reads=[("psA", 2 + i // 2)], writes=[("gdoa", i)])
                                fw.op("dve", lambda e, i=i, h=h, p1=p1: e.scalar_tensor_tensor(oa[:, i, :], p1, eb[:, h:h + 1], oa[:, i, :], ALU.mult, ALU.add),
                                      reads=[("psA", i // 2), "gdeb", ("gdoa", i)], writes=[("gdoa", i)])
                            fw.dma("sp", d["oscr"].ap()[tok0:tok0 + 64, hh * 512:(hh + 1) * 512].rearrange("p (h v) -> p h v", h=4), oa[:, :, 0:DV],
                                   reads=[("gdoa", i) for i in range(4)], writes=[("oscr", tok0 // 128, hh)])
                            fw.op("act", lambda e, hh=hh: e.copy(Rb[:, hh * 4:hh * 4 + 4, :], oa[:, :, DV:DA]),
                                  reads=[("gdoa", i) for i in range(4)], writes=[("gdRb", hh)])
                            for i, h in enumerate(hs):
                                fw.op("pe", lambda e, h=h, i=i: e.matmul(self.psO[:, i * DA:(i + 1) * DA], kd[:, h, :], vnew[:, h, :], start=True, stop=True),
                                      reads=["gdkd", ("gdvnew", hh)], writes=["psO"])
                            for i, h in enumerate(hs):
                                fw.op("dve", lambda e, h=h, i=i: e.scalar_tensor_tensor(S[:, h, :], S[:, h, :], cd[:, h:h + 1], self.psO[:, i * DA:(i + 1) * DA],
                                                                                       ALU.mult, ALU.add),
                                      reads=["gdS", "gdcd", "psO"], writes=["gdS"])
                            fw.op("act", lambda e, hh=hh: e.copy(Sb[:, hh * 4:hh * 4 + 4, :], S[:, hh * 4:hh * 4 + 4, :]),
                                  reads=["gdS"], writes=[("gdSb", hh)])
                        for h in range(NH):
                            fw.op("pe", lambda e, h=h: e.transpose(self.psT[:, h, 0:64], Rb[:, h, :], self.ident[0:64, 0:64]),
                                  reads=[("gdRb", 0), ("gdRb", 1), "ident"], writes=["psT"])
                        fw.op("act", lambda e: e.copy(RTs[:], self.psT[:, :, 0:64]), reads=["psT"], writes=["gdRTs"])
                        fw.dma("sp", d["rtscr"].ap()[:, :, gsl], RTs[:], reads=["gdRTs"], writes=[("rtscr", tok0 // 128)])
                    if b == 0:
                        self.chk(5)
                self.chk(6)
                fw.dma("sp", d["gd_in"].ap(), S[:].rearrange("p h a -> p (h a)"), reads=["gdS"], writes=["gd_in"])
                fw.allgather(d["gd_in"].ap(), d["gd_all"].ap(), reads=["gd_in"], writes=["gd_all"])
                fw.barrier()
            if self.stopped:
                fw.barrier()
                return
            with contextlib.ExitStack() as es:
                sb = lambda name, shape, dt: es.enter_context(nc.sbuf_tensor("sb_%s_%d" % (name, _uid()), list(shape), dt))
                Sall = sb("gdSall", [128, NCORES, DA], F32)
                st = sb("gdstt", [128, DV], F32)
                S0 = sb("gdS0", [128, DV], F32)
                PTs = sb("gdPTs", [128, 128], F32)
                gv = d["gd_all"].ap().rearrange("(c p) (h a) -> p c h a", p=128, h=NH)
                for h in range(NH):
                    fw.dma("sp", Sall[:], gv[:, :, h, :], reads=["gd_all"], writes=["gdSall"])
                    fw.op("dve", lambda e: e.memset(st[:], 0.0), writes=["gdstt"])
                    fw.op("dve", lambda e: e.memset(S0[:], 0.0), writes=["gdS0"])
                    for c in range(NCORES):
                        fw.op("dve", lambda e, c=c: e.scalar_tensor_tensor(S0[:], st[:], self.cmask[:, c:c + 1], S0[:], ALU.mult, ALU.add),
                              reads=["gdstt", "cmask", "gdS0"], writes=["gdS0"])
                        if c < NCORES - 1:
                            fw.op("pe", lambda e, c=c: e.transpose(self.psM[:, 0:128], Sall[:, c, DV:DA], self.identf[:]),
                                  reads=["gdSall", "identf"], writes=["psM"])
                            fw.op("act", lambda e: e.copy(PTs[:], self.psM[:, 0:128]), reads=["psM"], writes=["gdPTs"])
                            fw.op("pe", lambda e: e.matmul(self.psA[0][:, 0:128], PTs[:], st[:], start=True, stop=True),
                                  reads=["gdPTs", "gdstt"], writes=[("psA", 0)])
                            fw.op("dve", lambda e, c=c: e.tensor_tensor(st[:], self.psA[0][:, 0:128], Sall[:, c, 0:DV], ALU.add),
                                  reads=[("psA", 0), "gdSall"], writes=["gdstt"])
                    fw.op("act", lambda e, h=h: e.copy(S0b[:, h * DV:(h + 1) * DV], S0[:]), reads=["gdS0"], writes=["S0b"])
                fw.barrier()
            self.chk(7)
            self.mix_tail(li, NH, DV, None, S0b, d["gdn_w_in"].ap()[j][:, 3072:4096], d["gdn_w_out"].ap()[j],
                          d["gdn_norm_wt"].ap()[j:j + 1, :])

    def mix_tail(self, li, NH, DV, QET, S0b, wg_ap, wo_ap, nw_ap):
        nc = self.nc
        fw = self.fw
        d = self.d
        with contextlib.ExitStack() as es:
            sb = lambda name, shape, dt: es.enter_context(nc.sbuf_tensor("sb_%s_%d" % (name, _uid()), list(shape), dt))
            Wg = sb("mtWg", [128, KC, D], BF16)
            WOm = sb("mtWO", [128, KC, D], BF16)
            nw = sb("mtnw", [128, D], F32)
            ol = sb("mtol", [128, D], F32)
            sg = sb("mtsg", [128, D], F32)
            og = sb("mtog", [128, D], BF16)
            st = sb("mtst", [128, NH, 6], F32)
            mvh = sb("mtmv", [128, NH, 2], F32)
            ms = sb("mtms", [128, NH], F32)
            self.load_wout(Wg, wg_ap, "mtWg")
            self.load_wout(WOm, wo_ap, "mtWO")
            fw.dma("sp", nw[:], nw_ap.partition_broadcast(128), writes=["mtnw"])
            QK = [("QET", b) for b in range(4)]
            if QET is None:
                qst = sb("mtqst", [128, NH, 128], BF16)
            for t in range(NT):
                tsl = slice(t * 128, (t + 1) * 128)
                fw.dma("sp", ol[:], d["oscr"].ap()[t * 128:(t + 1) * 128, :], reads=[("oscr", t), ("oscr", t, 0), ("oscr", t, 1)], writes=["mtol"])
                if QET is None:
                    fw.dma("sp", qst[:], d["rtscr"].ap()[:, :, tsl], reads=[("rtscr", t)], writes=["mtqst"])
                for h in range(NH):
                    col = h * DV
                    lhs = QET[:, h, tsl] if QET is not None else qst[:, h, :]
                    fw.op("pe", lambda e, h=h, col=col, lhs=lhs: e.matmul(self.psA[col // 512][:, col % 512:col % 512 + DV], lhs,
                                                                          S0b[:, col:col + DV], start=True, stop=True),
                          reads=(QK if QET is not None else ["mtqst"]) + ["S0b"], writes=[("psA", col // 512)])
                for hf in range(2):
                    fw.op("dve", lambda e, hf=hf: e.tensor_tensor(ol[:, hf * 512:(hf + 1) * 512], ol[:, hf * 512:(hf + 1) * 512],
                                                                  self.psA[hf][:], ALU.add),
                          reads=["mtol", ("psA", hf)], writes=["mtol"])
                for hf in range(2):
                    for kc in range(KC):
                        fw.op("pe", lambda e, kc=kc, hf=hf: e.matmul(self.psA[2 + hf][:], self.XT[:, kc, tsl], Wg[:, kc, hf * 512:(hf + 1) * 512],
                                                                     start=(kc == 0), stop=(kc == KC - 1)),
                              reads=[("XT", t), "mtWg"] + [("mtWg", k) for k in range(1, KC)], writes=[("psA", 2 + hf)])
                    fw.op("act", lambda e, hf=hf: e.activation(sg[:, hf * 512:(hf + 1) * 512], self.psA[2 + hf][:], AF.Silu),
                          reads=[("psA", 2 + hf)], writes=["mtsg"])
                for h in range(NH):
                    fw.op("dve", lambda e, h=h: e.bn_stats(st[:, h, :], ol[:, h * DV:(h + 1) * DV]), reads=["mtol"], writes=["mtst"])
                    fw.op("dve", lambda e, h=h: e.bn_aggr(mvh[:, h, :], st[:, h, :]), reads=["mtst"], writes=["mtmv"])
                fw.op("dve", lambda e: e.tensor_tensor(ms[:], mvh[:, :, 0], mvh[:, :, 0], ALU.mult), reads=["mtmv"], writes=["mtms"])
                fw.op("dve", lambda e: e.tensor_tensor(ms[:], ms[:], mvh[:, :, 1], ALU.add), reads=["mtms", "mtmv"], writes=["mtms"])
                fw.op("act", lambda e: e.activation(ms[:], ms[:], AF.Sqrt, bias=RMS_EPS, scale=1.0), reads=["mtms"], writes=["mtms"])
                fw.op("dve", lambda e: e.reciprocal(ms[:], ms[:]), reads=["mtms"], writes=["mtms"])
                fw.op("dve", lambda e: e.tensor_tensor(ol[:].rearrange("p (h v) -> p h v", h=NH), ol[:].rearrange("p (h v) -> p h v", h=NH),
                                                       ms[:].unsqueeze(2).broadcast_to([128, NH, DV]), ALU.mult),
                      reads=["mtol", "mtms"], writes=["mtol"])
                fw.op("pool", lambda e: e.tensor_tensor(ol[:], ol[:], nw[:], ALU.mult), reads=["mtol", "mtnw"], writes=["mtol"])
                fw.op("dve", lambda e: e.tensor_tensor(og[:], ol[:], sg[:], ALU.mult), reads=["mtol", "mtsg"], writes=["mtog"])
                self.out_proj_ln(t, og, ["mtog"], WOm, "mtWO", li * 2)
            fw.barrier()

    def conv_ffn_jobs(self, layer):
        d = self.d
        fw = self.fw
        jobs = []
        wi = d["ffn_w_in"].ap()[layer]
        for j in range(NFC):
            dst = d["wffi"].ap()[layer, j].rearrange("p (k c) -> p k c", c=256)
            for half, col0 in enumerate((j * 128, DFF + j * 128)):
                jobs.append(lambda dst=dst, half=half, col0=col0, j=j: fw.dma(
                    "pool", dst[:, :, half * 128:(half + 1) * 128],
                    wi[:, col0:col0 + 128].rearrange("(k p) c -> p k c", p=128), writes=[("wffi", layer, j, half)]))
        wo_src = d["ffn_w_out"].ap()[layer].rearrange("(j p) n -> p j n", p=128)
        wo_dst = d["wffo"].ap()[layer].rearrange("p (j n) -> p j n", n=D)
        for part in range(2):
            js = slice(part * 11, (part + 1) * 11)
            jobs.append(lambda js=js, part=part: fw.dma("pool", wo_dst[:, js, :], wo_src[:, js, :], writes=[("wffo", layer, part)]))
        return jobs

    def conv_gdn_jobs(self):
        d = self.d
        fw = self.fw
        jobs = []
        wi = d["gdn_w_in"].ap()[0]
        for cc in range(24):
            dst = d["wgdn"].ap()[cc].rearrange("p (k c) -> p k c", c=128)
            jobs.append(lambda dst=dst, cc=cc: fw.dma("pool", dst, wi[:, cc * 128:(cc + 1) * 128].rearrange("(k p) c -> p k c", p=128),
                                                   writes=[("wgdn", cc)]))
        return jobs

    def run_jobs(self, n=None):
        k = len(self.pending) if n is None else min(n, len(self.pending))
        for _ in range(k):
            self.pending.pop(0)()

    def load_ln(self, li):
        fw = self.fw
        fw.dma("sp", self.lng[:], self.d["ln_g"].ap()[li:li + 1, :].partition_broadcast(128), writes=["lng"])
        fw.dma("sp", self.lnb[:], self.d["ln_b"].ap()[li:li + 1, :].partition_broadcast(128), writes=["lnb"])

    def ffn(self, layer):
        nc = self.nc
        fw = self.fw
        d = self.d
        TB = 512
        NTB = TPC // TB
        self.load_ln(layer * 2 + 1)
        with contextlib.ExitStack() as es:
            sb = lambda name, shape, dt: es.enter_context(nc.sbuf_tensor("sb_%s_%d" % (name, _uid()), list(shape), dt))
            WO = sb("WO", [128, NFC, D], BF16)
            NWB = 3
            Wb = [sb("Wb%d" % i, [128, KC, 256], BF16) for i in range(NWB)]
            cw = sb("cw", [128, 2 * NFC, 3], F32)
            carry = sb("carry", [128, 2 * NFC, 2], F32)
            hb = [sb("hb%d" % i, [128, 2 + TB], F32) for i in range(2)]
            acc = [sb("acc%d" % i, [128, TB], F32) for i in range(2)]
            gg = sb("gg", [128, TB], F32)
            G = sb("G", [128, NFC, TB], BF16)

            fw.dma("sp", cw[:], d["ffn_cw"].ap()[layer], writes=["cw"])
            wo_v = d["wffo"].ap()[layer].rearrange("p (j n) -> p j n", n=D)
            for part in range(2):
                js = slice(part * 11, (part + 1) * 11)
                fw.dma("sp", WO[:, js, :], wo_v[:, js, :], reads=[("wffo", layer, part)], writes=[("WO", j) for j in range(part * 11, part * 11 + 11)])
            wi = d["ffn_w_in"].ap()[layer]
            step = 0
            for tb in range(NTB):
                tsl = slice(tb * TB, (tb + 1) * TB)
                for j in range(NFC):
                    s = step % NWB
                    step += 1
                    self.run_jobs(1)
                    fw.dma("sp", Wb[s][:].rearrange("p k c -> p (k c)"), d["wffi"].ap()[layer, j],
                           reads=[("wffi", layer, j, 0), ("wffi", layer, j, 1)], writes=[("Wb", s), ("Wb2", s)])
                    for half in range(2):
                        pst = self.psA[half * 2 + (j % 2)]
                        pkey = ("psA", half * 2 + (j % 2))
                        for kc in range(KC):
                            fw.op("pe", lambda e, kc=kc, half=half, pst=pst, s=s: e.matmul(
                                pst[:], Wb[s][:, kc, half * 128:(half + 1) * 128], self.XT[:, kc, tsl],
                                start=(kc == 0), stop=(kc == KC - 1)),
                                reads=[("Wb", s), ("Wb2", s)] + [("XT", t) for t in range(tb * 4, tb * 4 + 4)], writes=[pkey])
                        ch = half * NFC + j
                        if tb == 0:
                            for kc in range(KC):
                                fw.op("pe", lambda e, kc=kc, half=half, s=s: e.matmul(
                                    self.psM[:, 0:2], Wb[s][:, kc, half * 128:(half + 1) * 128], self.XTH[:, kc, 2:4],
                                    start=(kc == 0), stop=(kc == KC - 1)),
                                    reads=[("Wb", s), ("Wb2", s), "XTH"], writes=["psM"])
                            fw.op("act", lambda e, half=half: e.copy(hb[half][:, 0:2], self.psM[:, 0:2]),
                                  reads=["psM"], writes=[("hb", half)])
                        else:
                            fw.op("act", lambda e, half=half, ch=ch: e.copy(hb[half][:, 0:2], carry[:, ch, :]),
                                  reads=[("carry", ch)], writes=[("hb", half)])
                        fw.op("act", lambda e, half=half, pst=pst: e.copy(hb[half][:, 2:2 + TB], pst[:]),
                              reads=[pkey], writes=[("hb", half)])
                        fw.op("act", lambda e, half=half, ch=ch: e.copy(carry[:, ch, :], hb[half][:, TB:TB + 2]),
                              reads=[("hb", half)], writes=[("carry", ch)])
                        a = acc[half]
                        fw.op("act", lambda e, half=half, ch=ch, a=a, pst=pst: e.activation(
                            a[:], pst[:], AF.Identity, scale=cw[:, ch, 2:3]),
                            reads=[pkey, "cw"], writes=[("acc", half)])
                        fw.op("dve", lambda e, half=half, ch=ch, a=a: e.scalar_tensor_tensor(
                            a[:], hb[half][:, 1:1 + TB], cw[:, ch, 1:2], a[:], ALU.mult, ALU.add),
                            reads=[("hb", half), "cw", ("acc", half)], writes=[("acc", half)])
                        fw.op("dve", lambda e, half=half, ch=ch, a=a: e.scalar_tensor_tensor(
                            a[:], hb[half][:, 0:TB], cw[:, ch, 0:1], a[:], ALU.mult, ALU.add),
                            reads=[("hb", half), "cw", ("acc", half)], writes=[("acc", half)])
                    fw.op("act", lambda e: e.activation(gg[:], acc[0][:], AF.Gelu),
                          reads=[("acc", 0)], writes=["gg"])
                    fw.op("dve", lambda e, j=j: e.tensor_tensor(G[:, j, :], gg[:], acc[1][:], ALU.mult),
                          reads=["gg", ("acc", 1)], writes=[("G", j)])
                for tt in range(TB // 128):
                    t = tb * (TB // 128) + tt
                    for j in range(NFC):
                        for hf in range(2):
                            fw.op("pe", lambda e, j=j, hf=hf, tt=tt: e.matmul(
                                self.psO[:, hf * 512:(hf + 1) * 512], G[:, j, tt * 128:(tt + 1) * 128],
                                WO[:, j, hf * 512:(hf + 1) * 512], start=(j == 0), stop=(j == NFC - 1)),
                                reads=[("G", j), ("WO", j)], writes=["psO"])
                    self.layer_norm_tile(t, self.psO[:], layer * 2 + 1, ["psO"])
                    self.make_xt(t)
            fw.barrier()


_CACHE = {}


def _host_consts(core):
    ident = np.eye(128, dtype=np.float32)
    hmask = np.zeros((128, NCORES, 32), np.float32)
    if core > 0:
        hmask[:, core - 1, :] = 1.0
    triu = np.triu(np.ones((128, 128), np.float32))
    cmask = np.zeros((128, NCORES), np.float32)
    cmask[:, core] = 1.0
    trilS = np.tril(np.ones((128, 128), np.float32), -1)
    return dict(ident=ident, hmask=hmask, triu=triu, cmask=cmask, trilS=trilS)


def _prep_inputs(inputs, names):
    x = np.ascontiguousarray(inputs["x"]).reshape(SEQ, D)
    shared = {}
    if "ffn_w_in" in names:
        shared["ffn_w_in"] = np.ascontiguousarray(inputs["ffn_w_in"])
        shared["ffn_w_out"] = np.ascontiguousarray(inputs["ffn_w_out"])
        cw = np.asarray(inputs["ffn_conv_w"])
        shared["ffn_cw"] = np.ascontiguousarray(cw.reshape(DEPTH, 3, 2 * NFC, 128).transpose(0, 3, 2, 1))
        shared["ln_g"] = np.ascontiguousarray(inputs["ln_g"]).reshape(DEPTH * 2, D)
        shared["ln_b"] = np.ascontiguousarray(inputs["ln_b"]).reshape(DEPTH * 2, D)
    if "gla_w_in" in names:
        shared["gla_w_in"] = np.ascontiguousarray(inputs["gla_w_in"])
        shared["gla_w_gk2"] = np.ascontiguousarray(inputs["gla_w_gk2"])
        shared["gla_b_gkT"] = np.ascontiguousarray(np.asarray(inputs["gla_b_gk"]).reshape(2, 4, 128).transpose(0, 2, 1))
        shared["gla_norm_wt"] = np.ascontiguousarray(np.tile(np.asarray(inputs["gla_norm_w"]), (1, 4)))
        shared["gla_w_out"] = np.ascontiguousarray(inputs["gla_w_out"])
    if "gdn_w_in" in names:
        shared["gdn_w_in"] = np.ascontiguousarray(inputs["gdn_w_in"])
        cwg = np.asarray(inputs["gdn_conv_w"])
        shared["gdn_cw"] = np.ascontiguousarray(cwg.reshape(1, 4, 24, 128).transpose(0, 3, 2, 1))
        shared["gdn_a_log"] = np.ascontiguousarray(inputs["gdn_a_log"])
        shared["gdn_dt_bias"] = np.ascontiguousarray(inputs["gdn_dt_bias"])
        shared["gdn_norm_wt"] = np.ascontiguousarray(np.tile(np.asarray(inputs["gdn_norm_w"]), (1, 8)))
        shared["gdn_w_out"] = np.ascontiguousarray(inputs["gdn_w_out"])
    if "sg_w_in" in names:
        shared["sg_w_in"] = np.ascontiguousarray(inputs["sg_w_in"])
        shared["sg_w_out"] = np.ascontiguousarray(inputs["sg_w_out"])
        shared["sg_wspT"] = np.ascontiguousarray(np.asarray(inputs["sg_w_sp"]).transpose(0, 3, 1, 2))
        shared["sg_bspT"] = np.ascontiguousarray(np.asarray(inputs["sg_b_sp"]).transpose(0, 2, 1))
        shared["sg_ln_g"] = np.ascontiguousarray(inputs["sg_ln_g"])
        shared["sg_ln_b"] = np.ascontiguousarray(inputs["sg_ln_b"])
    maps = []
    for c in range(NCORES):
        m = dict(shared)
        m["x"] = np.ascontiguousarray(x[c * TPC:(c + 1) * TPC])
        m.update(_host_consts(c))
        maps.append({k: v for k, v in m.items() if k in names})
    return maps


def run(inputs, stages):
    import time
    t0 = time.time()
    b = Builder(stages)
    nc = b.build()
    print("build s", time.time() - t0, flush=True)
    names = set(b.din.keys())
    maps = _prep_inputs(inputs, names)
    t0 = time.time()
    import os
    tr = bool(os.environ.get("KTRACE"))
    res = run_bass_kernel_spmd(nc, maps, core_ids=list(range(NCORES)), **({"trace": True} if tr else {}))
    print("run s", time.time() - t0, "exec_time_ns", getattr(res, "exec_time_ns", None), flush=True)
    out = np.concatenate([np.asarray(r["out"]) for r in res.results], axis=0)
    return out.reshape(1, SEQ, D).astype(np.float32)


def kernel(**inputs):
    return run(inputs, ("full",))
```
